# Optimizing a Trainium2 kernel written in Bass

```python
import math
import jax, jax.numpy as jnp
from jax import lax
import numpy as np

D_MODEL = 1024
BATCH = 4
SEQ = 8192
DEPTH = 1

HEAD_DIM = 64
ATTN_HEADS = 8
KV_HEADS = 2
Q_PER_KV = ATTN_HEADS // KV_HEADS
WINDOW = 128
BLOCK = 128
ATTN_WIDTH = ATTN_HEADS * HEAD_DIM
KV_WIDTH = KV_HEADS * HEAD_DIM
POOL_WINDOWS = (2, 4, 8, 16)
POOL_GROUPS = len(POOL_WINDOWS)
POOL_WIDTH = D_MODEL - ATTN_WIDTH
POOL_GROUP_DIM = POOL_WIDTH // POOL_GROUPS
MIX_WIDTH = ATTN_WIDTH + POOL_WIDTH
IN_WIDTH = ATTN_WIDTH + 2 * KV_WIDTH + POOL_WIDTH
D_FF = ((8 * D_MODEL // 3 + 127) // 128) * 128
N_MOD = 9
ALPHA = (2.0 * DEPTH) ** 0.25
BETA = (8.0 * DEPTH) ** -0.25
LN_EPS = 1e-5
NEG_INF = -1e30

kernel_name = "hybrid_swa_sink_pool_macaron_deepnorm_adaln"


def _ln_stats(x):
    xf = x.astype(jnp.float32)
    mu = jnp.mean(xf, axis=-1, keepdims=True)
    var = jnp.mean(jnp.square(xf - mu), axis=-1, keepdims=True)
    return (xf - mu) * lax.rsqrt(var + LN_EPS)


def layer_norm(x, g, b):
    y = _ln_stats(x) * g.astype(jnp.float32) + b.astype(jnp.float32)
    return y.astype(x.dtype)


def modulate(x, shift, scale):
    y = _ln_stats(x) * (1.0 + scale[:, None, :].astype(jnp.float32)) + shift[:, None, :].astype(jnp.float32)
    return y.astype(x.dtype)


def swiglu(h, w_gate, w_up, w_down):
    return (jax.nn.silu(h @ w_gate) * (h @ w_up)) @ w_down


def sliding_window_sink_attention(q, k, v, sinks):
    b, s = q.shape[0], q.shape[1]
    nb = s // BLOCK
    qb = q.reshape(b, nb, BLOCK, KV_HEADS, Q_PER_KV, HEAD_DIM)
    kb = k.reshape(b, nb, BLOCK, KV_HEADS, HEAD_DIM)
    vb = v.reshape(b, nb, BLOCK, KV_HEADS, HEAD_DIM)
    pad = ((0, 0), (1, 0), (0, 0), (0, 0), (0, 0))
    kcat = jnp.concatenate([jnp.pad(kb, pad)[:, :-1], kb], axis=2)
    vcat = jnp.concatenate([jnp.pad(vb, pad)[:, :-1], vb], axis=2)
    scores = jnp.einsum('bnqkgd,bnjkd->bnkgqj', qb, kcat).astype(jnp.float32) * (HEAD_DIM ** -0.5)
    qpos = jnp.arange(BLOCK)[:, None] + BLOCK
    kpos = jnp.arange(2 * BLOCK)[None, :]
    rel = qpos - kpos
    band = (rel >= 0) & (rel < WINDOW)
    blk = jnp.arange(nb)[:, None, None]
    valid = band[None] & ((blk > 0) | (kpos >= BLOCK)[None])
    scores = jnp.where(valid[None, :, None, None], scores, NEG_INF)
    sink = sinks.astype(jnp.float32).reshape(1, 1, KV_HEADS, Q_PER_KV, 1, 1)
    m = jnp.maximum(jnp.max(scores, axis=-1, keepdims=True), sink)
    p = jnp.exp(scores - m)
    probs = p / (jnp.sum(p, axis=-1, keepdims=True) + jnp.exp(sink - m))
    out = jnp.einsum('bnkgqj,bnjkd->bnqkgd', probs.astype(v.dtype), vcat)
    return out.reshape(b, s, ATTN_WIDTH)


def multiscale_pool(p, w_mix, ch_scale):
    b, s, _ = p.shape
    pf = p.astype(jnp.float32)
    cs = jnp.pad(jnp.cumsum(pf, axis=1), ((0, 0), (1, 0), (0, 0)))
    t = jnp.arange(s)
    pooled = []
    for g, w in enumerate(POOL_WINDOWS):
        sl = slice(g * POOL_GROUP_DIM, (g + 1) * POOL_GROUP_DIM)
        csg = cs[:, :, sl]
        lo = jnp.take(csg, jnp.maximum(t + 1 - w, 0), axis=1)
        cnt = jnp.minimum(t + 1, w).astype(jnp.float32)[None, :, None]
        pooled.append((csg[:, 1:] - lo) / cnt - pf[:, :, sl])
    pooled = jnp.stack(pooled, axis=2).astype(p.dtype)
    mixed = jnp.einsum('bsgc,gcd->bsgd', pooled, w_mix)
    return mixed.reshape(b, s, POOL_WIDTH) * ch_scale


def token_mix(h, w_in, sinks, pool_w_mix, pool_scale, w_out):
    b, s, _ = h.shape
    proj = h @ w_in
    q = proj[..., :ATTN_WIDTH].reshape(b, s, ATTN_HEADS, HEAD_DIM)
    k = proj[..., ATTN_WIDTH:ATTN_WIDTH + KV_WIDTH].reshape(b, s, KV_HEADS, HEAD_DIM)
    v = proj[..., ATTN_WIDTH + KV_WIDTH:ATTN_WIDTH + 2 * KV_WIDTH].reshape(b, s, KV_HEADS, HEAD_DIM)
    pin = proj[..., ATTN_WIDTH + 2 * KV_WIDTH:]
    a = sliding_window_sink_attention(q, k, v, sinks)
    m = multiscale_pool(pin, pool_w_mix, pool_scale)
    return jnp.concatenate([a, m], axis=-1) @ w_out


def setup_inputs(seed: int = 0) -> dict:
    key = jax.random.key(seed)
    ks = jax.random.split(key, 24)
    L, D, F = DEPTH, D_MODEL, D_FF
    nrm = lambda k, shape, s: jax.random.normal(k, shape, jnp.float32) * s
    col_scale = jnp.concatenate([
        jnp.ones((ATTN_WIDTH + KV_WIDTH,), jnp.float32),
        jnp.full((KV_WIDTH + POOL_WIDTH,), BETA, jnp.float32)])
    return {
        "x": nrm(ks[0], (BATCH, SEQ, D), 1.0),
        "c": nrm(ks[1], (BATCH, D), 1.0),
        "w_ada": nrm(ks[2], (L, D, N_MOD * D), 0.5 * D ** -0.5),
        "b_ada": nrm(ks[3], (L, N_MOD * D), 0.02),
        "ffn1_w_gate": nrm(ks[4], (L, D, F), D ** -0.5),
        "ffn1_w_up": nrm(ks[5], (L, D, F), BETA * D ** -0.5),
        "ffn1_w_down": nrm(ks[6], (L, F, D), BETA * F ** -0.5),
        "ln1_g": 1.0 + nrm(ks[7], (L, D), 0.02),
        "ln1_b": nrm(ks[8], (L, D), 0.02),
        "w_in": nrm(ks[9], (L, D, IN_WIDTH), D ** -0.5) * col_scale,
        "attn_sinks": nrm(ks[10], (L, ATTN_HEADS), 0.5),
        "pool_w_mix": nrm(ks[11], (L, POOL_GROUPS, POOL_GROUP_DIM, POOL_GROUP_DIM), POOL_GROUP_DIM ** -0.5),
        "pool_scale": 1.0 + nrm(ks[12], (L, POOL_WIDTH), 0.1),
        "w_out": nrm(ks[13], (L, MIX_WIDTH, D), BETA * MIX_WIDTH ** -0.5),
        "ln2_g": 1.0 + nrm(ks[14], (L, D), 0.02),
        "ln2_b": nrm(ks[15], (L, D), 0.02),
        "ffn2_w_gate": nrm(ks[16], (L, D, F), D ** -0.5),
        "ffn2_w_up": nrm(ks[17], (L, D, F), BETA * D ** -0.5),
        "ffn2_w_down": nrm(ks[18], (L, F, D), BETA * F ** -0.5),
        "ln3_g": 1.0 + nrm(ks[19], (L, D), 0.02),
        "ln3_b": nrm(ks[20], (L, D), 0.02),
    }


def reference(x, c, w_ada, b_ada, ffn1_w_gate, ffn1_w_up, ffn1_w_down, ln1_g, ln1_b,
              w_in, attn_sinks, pool_w_mix, pool_scale, w_out, ln2_g, ln2_b,
              ffn2_w_gate, ffn2_w_up, ffn2_w_down, ln3_g, ln3_b):
    b = x.shape[0]
    for l in range(DEPTH):
        mod = (jax.nn.silu(c) @ w_ada[l] + b_ada[l]).reshape(b, N_MOD, D_MODEL)
        sh1, sc1, g1 = mod[:, 0], mod[:, 1], mod[:, 2]
        sh2, sc2, g2 = mod[:, 3], mod[:, 4], mod[:, 5]
        sh3, sc3, g3 = mod[:, 6], mod[:, 7], mod[:, 8]
        h = modulate(x, sh1, sc1)
        y = swiglu(h, ffn1_w_gate[l], ffn1_w_up[l], ffn1_w_down[l])
        x = layer_norm(ALPHA * x + 0.5 * g1[:, None, :] * y, ln1_g[l], ln1_b[l])
        h = modulate(x, sh2, sc2)
        y = token_mix(h, w_in[l], attn_sinks[l], pool_w_mix[l], pool_scale[l], w_out[l])
        x = layer_norm(ALPHA * x + g2[:, None, :] * y, ln2_g[l], ln2_b[l])
        h = modulate(x, sh3, sc3)
        y = swiglu(h, ffn2_w_gate[l], ffn2_w_up[l], ffn2_w_down[l])
        x = layer_norm(ALPHA * x + 0.5 * g3[:, None, :] * y, ln3_g[l], ln3_b[l])
    return x
```

```python
import numpy as np
import ml_dtypes
from contextlib import ExitStack
import concourse.bass as bass
import concourse.mybir as mybir
from concourse.bass_utils import run_bass_kernel_spmd

F32 = mybir.dt.float32
BF16 = mybir.dt.bfloat16
ALU = mybir.AluOpType
AF = mybir.ActivationFunctionType

D = 1024
FF = 2816
NFC = 22
NGRP = 11
SEQ = 8192
NB_OWN = 32
TOK_OWN = 4096
TOK_IN = TOK_OWN + 128
ALPHA = 2.0 ** 0.25
LN_EPS = 1e-5
EPS_POST = LN_EPS / (ALPHA * ALPHA)
NEG = -30000.0
POOL_WINDOWS = (2, 4, 8, 16)
ENG = ("pe", "act", "dve", "pool", "sp")


class Buf:
    def __init__(self, name, overlaps=()):
        self.name = name
        self.last_w = None
        self.readers = {}
        self.overlaps = list(overlaps)


class Prog:
    def __init__(self, nc, es):
        self.nc = nc
        self.es = es
        self.sems = {}
        for e in ENG:
            self.sems[e] = es.enter_context(nc.semaphore("s_" + e))
        self.cnt = {e: 0 for e in ENG}
        self.dcnt = {}
        self.q = {e: [] for e in ENG}
        self.waited = {e: {} for e in ENG}

    def new_sem(self, name):
        self.sems[name] = self.es.enter_context(self.nc.semaphore(name))
        self.dcnt[name] = 0
        return name

    def op(self, eng, fn, reads=(), writes=(), dma=None, waits=()):
        need = {}

        def add(ev):
            if ev is None:
                return
            k, v = ev
            if need.get(k, 0) < v:
                need[k] = v

        for b in reads:
            add(b.last_w)
        for b in writes:
            add(b.last_w)
            for k, v in b.readers.items():
                add((k, v))
            for o in b.overlaps:
                add(o.last_w)
                for k, v in o.readers.items():
                    add((k, v))
        for ev in waits:
            add(ev)
        wl = []
        for k, v in need.items():
            if k == eng and eng == "pe":
                continue
            if self.waited[eng].get(k, 0) >= v:
                continue
            self.waited[eng][k] = v
            wl.append((k, v))
        if dma is not None:
            self.dcnt[dma] += 16
            ev = (dma, self.dcnt[dma])
        else:
            self.cnt[eng] += 1
            ev = (eng, self.cnt[eng])
        self.q[eng].append((wl, fn, ev))
        for b in reads:
            if b.readers.get(ev[0], 0) < ev[1]:
                b.readers[ev[0]] = ev[1]
        for b in writes:
            b.last_w = ev
            b.readers = {}
        return ev

    def flush(self, block):
        qs = self.q
        self.q = {e: [] for e in ENG}
        sems = self.sems

        def mk(eng):
            def body(e):
                for wl, fn, ev in qs[eng]:
                    for k, v in wl:
                        e.wait_ge(sems[k], v)
                    ins = fn(e)
                    ins.then_inc(sems[ev[0]], 1 if ev[0] in ENG else 16)
            return body

        block.tensor(mk("pe"))
        block.scalar(mk("act"))
        block.vector(mk("dve"))
        block.gpsimd(mk("pool"))
        block.sync(mk("sp"))


def build_program():
    nc = bass.Bass("TRN2", target_bir_lowering=False)

    def din(name, shape, dt=F32):
        return nc.dram_tensor(name, list(shape), dt, kind="ExternalInput").ap()

    x_c = din("x_c", [TOK_IN, D])
    c_c = din("c_c", [128, 8])
    w_ada = din("w_ada", [D, 9 * D])
    b_ada = din("b_ada", [1, 9 * D])
    wg = [din("ffn1_w_gate", [D, FF]), din("ffn2_w_gate", [D, FF])]
    wu = [din("ffn1_w_up", [D, FF]), din("ffn2_w_up", [D, FF])]
    wdn = [din("ffn1_w_down", [FF, D]), din("ffn2_w_down", [FF, D])]
    lng_d = [din("ln1_g", [1, D]), din("ln2_g", [1, D]), din("ln3_g", [1, D])]
    lnb_d = [din("ln1_b", [1, D]), din("ln2_b", [1, D]), din("ln3_b", [1, D])]
    w_in = din("w_in", [D, 1280])
    sinks_d = din("sinks_b", [128, 8])
    wmix_d = din("pool_w_mix", [4, 128, 128])
    pscale_d = din("pool_scale_p", [128, 4])
    w_out = din("w_out", [D, D])
    ident_bf_d = din("ident_bf", [128, 128], BF16)
    ident_f_d = din("ident_f", [128, 128])
    masks_d = din("masks", [128, 3, 128], BF16)
    amat_d = din("amat", [128, 16, 128], BF16)
    out_c = nc.dram_tensor("out_c", [TOK_OWN, D], F32, kind="ExternalOutput").ap()

    scr_gu = [nc.dram_tensor(f"scr_gu{f}", [NGRP, 128, 8, 2, 256], BF16).ap() for f in range(2)]
    scr_d = [nc.dram_tensor(f"scr_d{f}", [128, NFC, D], BF16).ap() for f in range(2)]
    scr_win = nc.dram_tensor("scr_win", [128, 8, 1280], BF16).ap()
    scr_woa = nc.dram_tensor("scr_woa", [128, 4, D], BF16).ap()
    scr_wop = nc.dram_tensor("scr_wop", [128, 4, D], BF16).ap()

    with ExitStack() as es:
        P = Prog(nc, es)

        def sb(name, shape, dt=F32, stack=es):
            return stack.enter_context(nc.sbuf_tensor("t_" + name, list(shape), dt))

        ident_bf = sb("ident_bf", [128, 128], BF16)
        ones_bf = sb("ones_bf", [128, 64], BF16)
        masks = sb("masks", [128, 3, 128], BF16)
        amat = sb("amat", [128, 16, 128], BF16)
        wmix = sb("wmix", [128, 4, 128], BF16)
        pscale = sb("pscale", [128, 4])
        sinkexp = sb("sinkexp", [128, 8])
        sinkb = sb("sinkb", [128, 4, 128])
        lng = [sb(f"lng{i}", [128, D]) for i in range(3)]
        lnb = [sb(f"lnb{i}", [128, D]) for i in range(3)]
        gate3 = sb("gate3", [128, D])
        msc = sb("msc", [128, 6, 8])

        B_const = Buf("const")
        B_gates = Buf("gates")
        B_msc = Buf("msc")
        B_sink = Buf("sink")

        s_const = P.new_sem("d_const")

        with ExitStack() as pes:
            ct = sb("ct", [128, 8], F32, pes)
            ident_f = sb("ident_f", [128, 128], F32, pes)
            cs = sb("cs", [128, 8], F32, pes)
            crep = sb("crep", [128, 8, 128], BF16, pes)
            ones_f = sb("ones_f", [1, 128], F32, pes)
            bada = sb("bada", [1, 9 * D], F32, pes)
            modb = sb("modb", [128, 9 * D], F32, pes)
            wada_t = [sb(f"wada{i}", [128, 8, 256], BF16, pes) for i in range(4)]
            wada_f = [sb(f"wadaf{i}", [128, 8, 256], F32, pes) for i in range(2)]
            B_wadaf = [Buf(f"wadaf{i}") for i in range(2)]
            s_wadaf = [P.new_sem(f"d_wadaf{i}") for i in range(2)]
            dtmp = sb("dtmp", [128, 8, 128], F32, pes)
            gates = [sb(f"gate{i}", [128, D], F32, pes) for i in range(3)]
            stg_in = [sb(f"stg_in{i}", [128, 2, D], F32, pes) for i in range(3)]
            stg_out = [sb(f"stg_out{i}", [128, 2, D], BF16, pes) for i in range(3)]
            B_si = [Buf(f"si{i}") for i in range(3)]
            B_so = [Buf(f"so{i}") for i in range(3)]
            s_si = [P.new_sem(f"d_si{i}") for i in range(3)]
            s_so = [P.new_sem(f"d_so{i}") for i in range(3)]
            modps = [pes.enter_context(nc.psum_tensor(f"modps{i}", [128, 512], F32)) for i in range(2)]
            B_modps = [Buf(f"modps{i}") for i in range(2)]
            B_wada = [Buf(f"wada{i}") for i in range(4)]
            s_wada = [P.new_sem(f"d_wada{i}") for i in range(4)]
            B_modb = Buf("modb")
            B_misc = Buf("misc")
            B_dtmp = Buf("dtmp")

            cast_fam = {"gu0": [], "gu1": [], "win": []}
            for f in range(2):
                cast_ops = cast_fam[f"gu{f}"]
                for j in range(NGRP):
                    for gu, wsrc in enumerate((wg[f], wu[f])):
                        src = wsrc.rearrange("(kc p) (j n) -> j p kc n", p=128, n=256)[j]
                        dst = scr_gu[f][j][:, :, gu, :]
                        cast_ops.append((dst, src))
            cast_fam["wd2"] = []
            for j in range(NGRP):
                cast_fam["wd2"].append((scr_d[1][:, 2 * j:2 * j + 2, :],
                                        wdn[1].rearrange("(fc p) d -> p fc d", p=128)[:, 2 * j:2 * j + 2, :]))
            w_in_v = w_in.rearrange("(kc p) n -> p kc n", p=128)
            cast_ops = cast_fam["win"]
            cast_ops.append((scr_win[:, :, 512:1280], w_in_v[:, :, 512:1280]))
            for kc in range(8):
                for half in range(2):
                    src = w_in[kc * 128:(kc + 1) * 128, half * 256:(half + 1) * 256].rearrange("p (c d) -> p c d", d=64)
                    dst = scr_win[:, kc, 0:512].rearrange("p (c h d) -> p c h d", h=2, d=64)[:, :, half, :]
                    cast_ops.append((dst, src))
            cast_ops.append((wmix[:], wmix_d.rearrange("g c d -> c g d")))
            const_loads = [
                (ident_bf[:], ident_bf_d), (ident_f[:], ident_f_d), (masks[:], masks_d), (amat[:], amat_d),
                (pscale[:], pscale_d), (sinkexp[:], sinks_d), (ct[:], c_c), (bada[:], b_ada),
            ]
            for i in range(3):
                const_loads.append((lng[i][:], lng_d[i].partition_broadcast(128)))
                const_loads.append((lnb[i][:], lnb_d[i].partition_broadcast(128)))
            for dst, src in const_loads:
                P.op("sp", lambda e, dst=dst, src=src: e.dma_start(out=dst, in_=src), dma=s_const)
            EV_CONST = (s_const, P.dcnt[s_const])
            B_const.last_w = EV_CONST

            P.op("dve", lambda e: e.memset(ones_bf[:], 1.0), writes=[B_misc])
            P.op("dve", lambda e: e.memset(ones_f[:], 1.0), writes=[B_misc])
            P.op("act", lambda e: e.activation(out=cs[:], in_=ct[:], func=AF.Silu), reads=[B_const], writes=[B_misc])
            P.op("dve", lambda e: e.tensor_copy(out=crep[:], in_=cs[:].unsqueeze(2).to_broadcast([128, 8, 128])),
                 reads=[B_misc], writes=[B_misc])
            P.op("act", lambda e: e.activation(out=sinkexp[:], in_=sinkexp[:], func=AF.Exp), reads=[B_const], writes=[B_sink])
            for kv in range(2):
                P.op("dve", lambda e, kv=kv: e.tensor_copy(
                    out=sinkb[kv * 64:(kv + 1) * 64, :, :],
                    in_=sinkexp[kv * 64:(kv + 1) * 64, kv * 4:(kv + 1) * 4].unsqueeze(2).to_broadcast([64, 4, 128])),
                    reads=[B_sink], writes=[B_sink])

            w_ada_v = w_ada.rearrange("(kc p) n -> p kc n", p=128)
            for j in range(36):
                s = j % 4
                pm = j % 2
                if j % 2 == 0:
                    P.op("pool", lambda e, s=s, j=j: e.dma_start(out=wada_t[s][:], in_=w_ada_v[:, :, j * 256:(j + 1) * 256]),
                         writes=[B_wada[s]], dma=s_wada[s])
                else:
                    sf = (j // 2) % 2
                    P.op("sp", lambda e, sf=sf, j=j: e.dma_start(out=wada_f[sf][:], in_=w_ada_v[:, :, j * 256:(j + 1) * 256]),
                         writes=[B_wadaf[sf]], dma=s_wadaf[sf])
                    if sf == 0:
                        P.op("dve", lambda e, sf=sf, s=s: e.tensor_copy(out=wada_t[s][:], in_=wada_f[sf][:]),
                             reads=[B_wadaf[sf]], writes=[B_wada[s]])
                    else:
                        P.op("act", lambda e, sf=sf, s=s: e.activation(out=wada_t[s][:], in_=wada_f[sf][:], func=AF.Copy),
                             reads=[B_wadaf[sf]], writes=[B_wada[s]])

                def mm(e, s=s, j=j, pm=pm):
                    for kc in range(8):
                        e.matmul(modps[pm][:, 0:256], lhsT=crep[:, kc, :], rhs=wada_t[s][:, kc, :], start=(kc == 0), stop=False)
                    return e.matmul(modps[pm][:, 0:256], lhsT=ones_f[:], rhs=bada[:, j * 256:(j + 1) * 256], start=False, stop=True)
                P.op("pe", mm, reads=[B_wada[s], B_misc, B_const], writes=[B_modps[pm]])
                P.op("act", lambda e, pm=pm, j=j: e.activation(out=modb[:, j * 256:(j + 1) * 256], in_=modps[pm][:, 0:256], func=AF.Copy),
                     reads=[B_modps[pm]], writes=[B_modb])

            gsc = (0.5 / ALPHA, 1.0 / ALPHA, 0.5 / ALPHA)
            for i in range(3):
                off = (3 * i + 2) * D
                P.op("dve", lambda e, i=i, off=off: e.tensor_scalar_mul(out=gates[i][:], in0=modb[:, off:off + D], scalar1=gsc[i]),
                     reads=[B_modb], writes=[B_gates])
            P.op("dve", lambda e: e.tensor_copy(out=gate3[:], in_=gates[2][:]), reads=[B_gates], writes=[B_gates])
            for i in range(3):
                for k in range(2):
                    off = (3 * i + k) * D
                    P.op("dve", lambda e, off=off: e.tensor_tensor(
                        out=dtmp[:], in0=modb[:, off:off + D].rearrange("p (k n) -> p k n", n=128),
                        in1=ident_f[:].unsqueeze(1).to_broadcast([128, 8, 128]), op=ALU.mult),
                        reads=[B_modb, B_const], writes=[B_dtmp])
                    P.op("dve", lambda e, i=i, k=k: e.tensor_reduce(
                        out=msc[:, 2 * i + k, :], in_=dtmp[:], axis=mybir.AxisListType.X, op=ALU.add),
                        reads=[B_dtmp], writes=[B_msc])
                P.op("dve", lambda e, i=i: e.tensor_scalar_add(out=msc[:, 2 * i + 1, :], in0=msc[:, 2 * i + 1, :], scalar1=1.0),
                     reads=[B_msc], writes=[B_msc])
            fold = []
            for f in range(1):
                for j in range(NGRP):
                    fold.append(([(slice(0, 128), r_, wdn[f][(2 * j + r_) * 128:(2 * j + r_ + 1) * 128, :]) for r_ in range(2)],
                                 2 * f, None, scr_d[f][:, 2 * j:2 * j + 2, :]))
            for cp in range(2):
                srcs = []
                for r_ in range(2):
                    c = 2 * cp + r_
                    srcs.append((slice(0, 64), r_, w_out[c * 64:(c + 1) * 64, :]))
                    srcs.append((slice(64, 128), r_, w_out[256 + c * 64:256 + (c + 1) * 64, :]))
                fold.append((srcs, 1, None, scr_woa[:, 2 * cp:2 * cp + 2, :]))
            for gp in range(2):
                srcs = [(slice(0, 128), r_, w_out[512 + (2 * gp + r_) * 128:512 + (2 * gp + r_ + 1) * 128, :]) for r_ in range(2)]
                fold.append((srcs, 1, (2 * gp, 2 * gp + 1), scr_wop[:, 2 * gp:2 * gp + 2, :]))
            fold_ev = []
            for i, (srcs, gi, pgs, dst) in enumerate(fold):
                k = i % 3
                for sl, r_, src in srcs:
                    P.op("sp", lambda e, k=k, sl=sl, r_=r_, src=src: e.dma_start(out=stg_in[k][sl, r_, :], in_=src),
                         writes=[B_si[k]], dma=s_si[k])
                for r_ in range(2):
                    if pgs is None:
                        P.op("dve", lambda e, k=k, gi=gi, r_=r_: e.tensor_tensor(out=stg_out[k][:, r_, :], in0=stg_in[k][:, r_, :],
                                                                                 in1=gates[gi][:], op=ALU.mult),
                             reads=[B_si[k], B_gates], writes=[B_so[k]])
                    else:
                        P.op("dve", lambda e, k=k, gi=gi, r_=r_, pg=pgs[r_]: e.scalar_tensor_tensor(
                            out=stg_out[k][:, r_, :], in0=stg_in[k][:, r_, :], scalar=pscale[:, pg:pg + 1], in1=gates[gi][:],
                            op0=ALU.mult, op1=ALU.mult),
                            reads=[B_si[k], B_gates, B_const], writes=[B_so[k]])
                ev = P.op("act", lambda e, k=k, dst=dst: e.dma_start(out=dst, in_=stg_out[k][:]), reads=[B_so[k]], dma=s_so[k])
                fold_ev.append(ev)
            P.op("sp", lambda e: e.nop(), waits=fold_ev[-3:])
            with nc.Block() as blk:
                P.flush(blk)

        xt = [sb(f"xt{i}", [128, 4, D]) for i in range(2)]
        rbuf = [sb(f"rbuf{i}", [128, D]) for i in range(4)]
        n2b = [sb(f"n2b{i}", [128, D], BF16) for i in range(4)]
        hTa = sb("hTa", [128, 8, 512], BF16)
        hTb = sb("hTb", [128, 8, 512], BF16)
        sg = [sb(f"sg{i}", [128, 512], BF16) for i in range(2)]
        NRING = 2
        wgu = [sb(f"wgu{i}", [128, 8, 2, 256], BF16) for i in range(NRING)]
        u1 = sb("u1", [128, NFC * D], BF16)
        u2 = sb("u2", [128, NFC * 512], BF16)
        kT = sb("kT", [128, 8, 128], BF16)
        vp = sb("vp", [128, 8, 640], BF16)
        den = sb("den", [128, 512])
        stt = sb("stt", [128, 4, 2, 6])
        mv = sb("mv", [128, 4, 2])
        rs = sb("rs", [128, 4, 2])
        stt2 = sb("stt2", [128, 4, 2, 6])
        mv2 = sb("mv2", [128, 4, 2])
        rs2 = sb("rs2", [128, 4, 2])
        print("sbuf bytes remaining/partition:", nc.sbuf_bytes_remaining)

        ps = [es.enter_context(nc.psum_tensor(f"ps{i}", [128, 512], F32)) for i in range(7)]
        tp = es.enter_context(nc.psum_tensor("tp", [128, 2, 512], BF16))
        B_ps = [Buf(f"ps{i}") for i in range(7)]
        B_tp = Buf("tp")

        wd_v = u1[:].rearrange("p (f n) -> p f n", n=D)
        win_v = u1[:, 0:10240].rearrange("p (k n) -> p k n", n=1280)
        woa_v = u1[:, 10240:14336].rearrange("p (h n) -> p h n", n=D)
        wop_v = u1[:, 14336:18432].rearrange("p (g n) -> p g n", n=D)
        aT_v = u2[:].rearrange("p (f n) -> p f n", n=512)
        qT_v = u2[:, 0:2048].rearrange("p (c n) -> p c n", n=512)
        E_v = [u2[:, 2048 + i * 2048: 4096 + i * 2048].rearrange("p (a b n) -> p a b n", a=2, b=2) for i in range(2)]
        attn_v = [u2[:, 6144 + i * 512: 6656 + i * 512].rearrange("p (c n) -> p c n", n=128) for i in range(2)]
        pooled_v = u2[:, 7168:7680].rearrange("p (g n) -> p g n", n=128)
        mixT_v = u2[:, 7680:8192].rearrange("p (g n) -> p g n", n=128)

        B_wd = [Buf(f"wd{j}") for j in range(NGRP)]
        B_win = Buf("win", overlaps=B_wd[0:5])
        B_woa = Buf("woa", overlaps=B_wd[5:7])
        B_wop = Buf("wop", overlaps=B_wd[7:9])
        for j in range(0, 5):
            B_wd[j].overlaps.append(B_win)
        for j in range(5, 7):
            B_wd[j].overlaps.append(B_woa)
        for j in range(7, 9):
            B_wd[j].overlaps.append(B_wop)
        s_wd = [P.new_sem(f"d_wd{j}") for j in range(NGRP)]
        B_aT = [Buf(f"aT{f}") for f in range(NFC)]
        B_att = {n: Buf(n, overlaps=B_aT) for n in ("qT", "E0", "E1", "attn0", "attn1", "pooled", "mixT")}
        for b in B_aT:
            b.overlaps.extend(B_att.values())
        B_wgu = [Buf(f"wgu{i}") for i in range(NRING)]
        s_wgu = [P.new_sem(f"d_wgu{i}") for i in range(NRING)]
        B_x = [[Buf(f"x{p}_{b}") for b in range(4)] for p in range(2)]
        s_x = [P.new_sem(f"d_x{p}") for p in range(2)]
        B_r = [Buf(f"r{i}") for i in range(4)]
        s_r = [P.new_sem(f"d_r{i}") for i in range(4)]
        B_n2 = [Buf(f"n2b{i}") for i in range(4)]
        B_hTa = [Buf(f"hTa{b}") for b in range(4)]
        B_hTb = [Buf(f"hTb{b}") for b in range(4)]
        B_sg = [Buf(f"sg{i}") for i in range(2)]
        B_kT = [Buf(f"kT{i}") for i in range(8)]
        B_vp = [Buf(f"vp{i}") for i in range(8)]
        B_den = Buf("den")
        B_st = [Buf(f"st{b}") for b in range(4)]
        B_st2 = [Buf(f"st2{b}") for b in range(4)]
        B_scr = Buf("scr")
        B_scr_gu0 = [Buf(f"scr_gu0_{j}") for j in range(NGRP)]
        s_cg0 = [P.new_sem(f"d_cg0_{j}") for j in range(NGRP)]
        B_scr_gu1 = Buf("scr_gu1")
        s_cg1 = P.new_sem("d_cg1")
        B_scr_wd2 = Buf("scr_wd2")
        s_cwd2 = P.new_sem("d_cwd2")
        B_scr_win = Buf("scr_win")
        s_cwin = P.new_sem("d_cwin")
        B_wmix = Buf("wmix")
        B_misc_main = Buf("misc_main")
        out_events = []

        gu_reqs = []
        gu_state = {"issued": 0, "cons": 0, "slot": 0}
        r_ctr = {"sg": 0}

        def issue_gu(upto):
            while gu_state["issued"] < min(upto, len(gu_reqs)):
                i = gu_state["issued"]
                f, j = gu_reqs[i]
                s = i % NRING
                P.op("sp", lambda e, f=f, j=j, s=s: e.dma_start(out=wgu[s][:], in_=scr_gu[f][j]),
                     reads=[B_scr_gu0[j] if f == 0 else B_scr_gu1], writes=[B_wgu[s]], dma=s_wgu[s])
                gu_state["issued"] += 1

        def load_wd_slot(f, j):
            P.op("sp", lambda e, f=f, j=j: e.dma_start(out=wd_v[:, 2 * j:2 * j + 2, :], in_=scr_d[f][:, 2 * j:2 * j + 2, :]),
                 reads=[B_scr if f == 0 else B_scr_wd2], writes=[B_wd[j]], dma=s_wd[j])

        def load_mixer_w():
            P.op("sp", lambda e: e.dma_start(out=win_v, in_=scr_win), reads=[B_scr_win], writes=[B_win], dma=s_wd[0])
            P.op("sp", lambda e: e.dma_start(out=woa_v, in_=scr_woa), reads=[B_scr], writes=[B_woa], dma=s_wd[5])
            P.op("sp", lambda e: e.dma_start(out=wop_v, in_=scr_wop), reads=[B_scr], writes=[B_wop], dma=s_wd[7])

        def stats_gen(src_ap, st_t, mv_t, rs_t, b, B_src, B_s, eps):
            for h in range(2):
                P.op("dve", lambda e, h=h: e.bn_stats(out=st_t[:, b, h, :], in_=src_ap[:, h * 512:(h + 1) * 512]),
                     reads=[B_src], writes=[B_s])
            P.op("dve", lambda e: e.bn_aggr(out=mv_t[:, b, :], in_=st_t[:, b, :, :].rearrange("p a s -> p (a s)")), reads=[B_s], writes=[B_s])
            P.op("dve", lambda e: e.tensor_scalar_add(out=rs_t[:, b, 0:1], in0=mv_t[:, b, 1:2], scalar1=eps),
                 reads=[B_s], writes=[B_s])
            yield
            P.op("act", lambda e: e.activation(out=rs_t[:, b, 0:1], in_=rs_t[:, b, 0:1], func=AF.Ln), reads=[B_s], writes=[B_s])
            P.op("act", lambda e: e.activation(out=rs_t[:, b, 0:1], in_=rs_t[:, b, 0:1], func=AF.Exp, scale=-0.5),
                 reads=[B_s], writes=[B_s])
            yield
            P.op("dve", lambda e: e.tensor_scalar(out=rs_t[:, b, 1:2], in0=mv_t[:, b, 0:1], scalar1=rs_t[:, b, 0:1],
                                                  scalar2=-1.0, op0=ALU.mult, op1=ALU.mult), reads=[B_s], writes=[B_s])

        def modulate_gen(b, src_ap, B_src):
            yield from stats_gen(src_ap, stt2, mv2, rs2, b, B_src, B_st2[b], LN_EPS)
            P.op("act", lambda e: e.activation(out=n2b[b][:], in_=src_ap, func=AF.Identity,
                                               scale=rs2[:, b, 0:1], bias=rs2[:, b, 1:2]),
                 reads=[B_src, B_st2[b]], writes=[B_n2[b]])

        def m1_tile_gen(nb, row0):
            for b in range(nb):
                P.op("pool", lambda e, b=b: e.dma_start(out=rbuf[b][:], in_=x_c[row0 + b * 128:row0 + (b + 1) * 128, :]),
                     writes=[B_r[b]], dma=s_r[b])
            yield
            for b in range(nb):
                for h in range(2):
                    P.op("dve", lambda e, b=b, h=h: e.bn_stats(out=stt2[:, b, h, :], in_=rbuf[b][:, h * 512:(h + 1) * 512]),
                         reads=[B_r[b]], writes=[B_st2[b]])
                P.op("dve", lambda e, b=b: e.bn_aggr(out=mv2[:, b, :], in_=stt2[:, b, :, :].rearrange("p a s -> p (a s)")),
                     reads=[B_st2[b]], writes=[B_st2[b]])
                yield
            P.op("dve", lambda e: e.tensor_scalar_add(out=rs2[:, 0:nb, 0:1], in0=mv2[:, 0:nb, 1:2], scalar1=LN_EPS),
                 reads=B_st2[0:nb], writes=B_st2[0:nb])
            yield
            P.op("act", lambda e: e.activation(out=rs2[:, 0:nb, 0:1], in_=rs2[:, 0:nb, 0:1], func=AF.Ln), reads=B_st2[0:nb], writes=B_st2[0:nb])
            P.op("act", lambda e: e.activation(out=rs2[:, 0:nb, 0:1], in_=rs2[:, 0:nb, 0:1], func=AF.Exp, scale=-0.5),
                 reads=B_st2[0:nb], writes=B_st2[0:nb])
            yield
            P.op("dve", lambda e: e.scalar_tensor_tensor(out=rs2[:, 0:nb, 1:2], in0=mv2[:, 0:nb, 0:1], scalar=-1.0, in1=rs2[:, 0:nb, 0:1],
                                                         op0=ALU.mult, op1=ALU.mult), reads=B_st2[0:nb], writes=B_st2[0:nb])
            yield
            for b in range(nb):
                P.op("act", lambda e, b=b: e.activation(out=n2b[b][:], in_=rbuf[b][:], func=AF.Identity,
                                                        scale=rs2[:, b, 0:1], bias=rs2[:, b, 1:2]),
                     reads=[B_r[b], B_st2[b]], writes=[B_n2[b]])
                yield

        def transpose_evac(nb, mi, hdst, B_hdst):
            T = nb * 128
            for r in range(4):
                def tr(e, r=r):
                    for j in range(2):
                        kc = 2 * r + j
                        for b in range(nb):
                            ins = e.transpose(out=tp[:, j, b * 128:(b + 1) * 128], in_=n2b[b][:, kc * 128:(kc + 1) * 128],
                                              identity=ident_bf[:])
                    return ins
                P.op("pe", tr, reads=B_n2[0:nb] + [B_const], writes=[B_tp])
                for j in range(2):
                    kc = 2 * r + j
                    P.op("act", lambda e, j=j, kc=kc: e.activation(out=hdst[:, kc, 0:T], in_=tp[:, j, 0:T], func=AF.Identity,
                                                                   scale=msc[:, 2 * mi + 1, kc:kc + 1], bias=msc[:, 2 * mi, kc:kc + 1]),
                         reads=[B_tp, B_msc], writes=B_hdst[0:nb])

        def post_ln_evac(b, yb, par, gated=False):
            r = rbuf[b]
            for h in range(2):
                hs = slice(h * 512, (h + 1) * 512)
                if gated:
                    P.op("dve", lambda e, h=h, hs=hs: e.tensor_tensor(out=r[:, hs], in0=ps[yb[h]][:], in1=gate3[:, hs], op=ALU.mult),
                         reads=[B_ps[yb[h]], B_gates], writes=[B_r[b]])
                    P.op("dve", lambda e, hs=hs: e.tensor_tensor(out=r[:, hs], in0=r[:, hs], in1=xt[par][:, b, hs], op=ALU.add),
                         reads=[B_r[b], B_x[par][b]], writes=[B_r[b]])
                else:
                    P.op("dve", lambda e, h=h, hs=hs: e.tensor_tensor(out=r[:, hs], in0=ps[yb[h]][:], in1=xt[par][:, b, hs], op=ALU.add),
                         reads=[B_ps[yb[h]], B_x[par][b]], writes=[B_r[b]])

        def post_ln_gen(b, li, final, row0, par, then_mod):
            r = rbuf[b]
            yield from stats_gen(r, stt, mv, rs, b, B_r[b], B_st[b], EPS_POST)
            P.op("act", lambda e: e.activation(out=r[:], in_=r[:], func=AF.Identity, scale=rs[:, b, 0:1], bias=rs[:, b, 1:2]),
                 reads=[B_r[b], B_st[b]], writes=[B_r[b]])
            yield
            P.op("dve", lambda e: e.tensor_tensor(out=r[:], in0=r[:], in1=lng[li][:], op=ALU.mult),
                 reads=[B_r[b], B_const], writes=[B_r[b]])
            if final:
                P.op("pool", lambda e: e.tensor_tensor(out=r[:], in0=r[:], in1=lnb[li][:], op=ALU.add),
                     reads=[B_r[b], B_const], writes=[B_r[b]])
                ev = P.op("pool", lambda e: e.dma_start(out=out_c[row0 + b * 128: row0 + (b + 1) * 128, :], in_=r[:]),
                          reads=[B_r[b]], dma=s_r[b])
                out_events.append(ev)
                return
            P.op("pool", lambda e: e.tensor_tensor(out=xt[par][:, b, :], in0=r[:], in1=lnb[li][:], op=ALU.add),
                 reads=[B_r[b], B_const], writes=[B_x[par][b]])
            yield
            if then_mod:
                yield from modulate_gen(b, xt[par][:, b, :], B_x[par][b])

        live = []

        def tick():
            for g in list(live):
                try:
                    next(g)
                except StopIteration:
                    live.remove(g)

        def run_until_done(gens):
            while any(g in live for g in gens):
                tick()

        def drain():
            while live:
                tick()

        def spawn(g):
            live.append(g)
            return g

        def gateup(f, nb, hsrc, B_hsrc, hook=None):
            T = nb * 128
            for fc in range(NFC):
                s2 = fc % 2
                if s2 == 0:
                    gi = gu_state["cons"]
                    gu_state["cons"] += 1
                    issue_gu(gi + NRING)
                    gu_state["slot"] = gi % NRING
                slot = gu_state["slot"]
                pg, pu = fc % 2, 2 + fc % 2

                def mm(e, slot=slot, s2=s2, pg=pg, pu=pu):
                    for gu, pb in ((0, pg), (1, pu)):
                        for kc in range(8):
                            ins = e.matmul(ps[pb][:, 0:T], lhsT=wgu[slot][:, kc, gu, s2 * 128:(s2 + 1) * 128],
                                           rhs=hsrc[:, kc, 0:T], start=(kc == 0), stop=(kc == 7))
                    return ins
                P.op("pe", mm, reads=[B_wgu[slot]] + B_hsrc[0:nb], writes=[B_ps[pg], B_ps[pu]])
                isg = r_ctr["sg"] % 2
                r_ctr["sg"] += 1
                P.op("act", lambda e, pg=pg, isg=isg: e.activation(out=sg[isg][:, 0:T], in_=ps[pg][:, 0:T], func=AF.Silu),
                     reads=[B_ps[pg]], writes=[B_sg[isg]])
                P.op("dve", lambda e, pu=pu, isg=isg, fc=fc: e.tensor_tensor(out=aT_v[:, fc, 0:T], in0=ps[pu][:, 0:T],
                                                                             in1=sg[isg][:, 0:T], op=ALU.mult),
                     reads=[B_ps[pu], B_sg[isg]], writes=[B_aT[fc]])
                tick()
                if fc % 2 == 1:
                    load_wd_slot(f, fc // 2)
                if nb == 4:
                    bg_cast_step(2)
                if hook is not None:
                    hook()

        def down(nb, li, final, row0, par):
            gens = []
            for b in range(nb):
                for h in range(2):
                    def mm(e, b=b, h=h):
                        for fc in range(NFC):
                            ins = e.matmul(ps[4 + h][:], lhsT=aT_v[:, fc, b * 128:(b + 1) * 128],
                                           rhs=wd_v[:, fc, h * 512:(h + 1) * 512], start=(fc == 0), stop=(fc == NFC - 1))
                        return ins
                    P.op("pe", mm, reads=B_aT + B_wd, writes=[B_ps[4 + h]])
                post_ln_evac(b, (4, 5), par, gated=final)
                gens.append(spawn(post_ln_gen(b, li, final, row0, par, False)))
                tick()
                tick()
            return gens

        def w_in_proj(nb, gbs, with_q, hsrc, B_hsrc):
            T = nb * 128
            chunks = ([0, 1, 2, 3] if with_q else []) + [4]
            for ci, c in enumerate(chunks):
                pb = ci % 2

                def mm(e, c=c, pb=pb):
                    for kc in range(8):
                        ins = e.matmul(ps[pb][:, 0:T], lhsT=win_v[:, kc, c * 128:(c + 1) * 128], rhs=hsrc[:, kc, 0:T],
                                       start=(kc == 0), stop=(kc == 7))
                    return ins
                P.op("pe", mm, reads=[B_win] + B_hsrc[0:nb], writes=[B_ps[pb]])
                if c < 4:
                    P.op("act", lambda e, c=c, pb=pb: e.activation(out=qT_v[:, c, 0:T], in_=ps[pb][:, 0:T], func=AF.Copy),
                         reads=[B_ps[pb]], writes=[B_att["qT"]])
                else:
                    for b in range(nb):
                        sl = gbs[b] % 8
                        P.op("act", lambda e, b=b, sl=sl, pb=pb: e.activation(out=kT[:, sl, :], in_=ps[pb][:, b * 128:(b + 1) * 128], func=AF.Copy),
                             reads=[B_ps[pb]], writes=[B_kT[sl]])
                tick()
            for b in range(nb):
                sl = gbs[b] % 8
                pv_, pp_ = (2, 3) if b % 2 == 0 else (6, 4)

                def mm(e, b=b, pv_=pv_, pp_=pp_):
                    for kc in range(8):
                        e.matmul(ps[pv_][:, 0:128], lhsT=hsrc[:, kc, b * 128:(b + 1) * 128], rhs=win_v[:, kc, 640:768],
                                 start=(kc == 0), stop=(kc == 7))
                        ins = e.matmul(ps[pp_][:], lhsT=hsrc[:, kc, b * 128:(b + 1) * 128], rhs=win_v[:, kc, 768:1280],
                                       start=(kc == 0), stop=(kc == 7))
                    return ins
                P.op("pe", mm, reads=[B_win, B_hsrc[b]], writes=[B_ps[pv_], B_ps[pp_]])
                P.op("dve", lambda e, sl=sl, pv_=pv_: e.tensor_copy(out=vp[:, sl, 0:128], in_=ps[pv_][:, 0:128]),
                     reads=[B_ps[pv_]], writes=[B_vp[sl]])
                P.op("act", lambda e, sl=sl, pp_=pp_: e.activation(out=vp[:, sl, 128:640], in_=ps[pp_][:], func=AF.Copy),
                     reads=[B_ps[pp_]], writes=[B_vp[sl]])
                tick()

        stb = {(0, 0): 0, (0, 1): 1, (1, 0): 2, (1, 1): 3}

        def mix_scores(b, gb):
            slc, slp = gb % 8, (gb - 1) % 8
            first = (gb == 1)
            iE = b % 2
            BE = B_att[f"E{iE}"]
            for jj, sl in ((0, slp), (1, slc)):
                mi = (2 if first else 1) if jj == 0 else 0
                for kv in range(2):
                    pb = stb[(kv, jj)]

                    def mm(e, kv=kv, sl=sl, pb=pb, mi=mi):
                        e.matmul(ps[pb][:].rearrange("p (c q) -> p c q", c=4), lhsT=kT[kv * 64:(kv + 1) * 64, sl, :],
                                 rhs=qT_v[kv * 64:(kv + 1) * 64, :, b * 128:(b + 1) * 128], start=True, stop=False)
                        return e.matmul(ps[pb][:].rearrange("p (c q) -> p c q", c=4), lhsT=ident_bf[:],
                                        rhs=masks[:, mi, :].unsqueeze(1).to_broadcast([128, 4, 128]), start=False, stop=True)
                    P.op("pe", mm, reads=[B_kT[sl], B_att["qT"], B_const], writes=[B_ps[pb]])
                    P.op("act", lambda e, kv=kv, jj=jj, pb=pb: e.activation(out=E_v[iE][:, kv, jj, :], in_=ps[pb][:], func=AF.Exp, scale=0.125),
                         reads=[B_ps[pb]], writes=[BE])

        def mix_pv(b, gb):
            slc, slp = gb % 8, (gb - 1) % 8
            first = (gb == 1)
            iE = b % 2
            BE = B_att[f"E{iE}"]

            def mm(e):
                for kv in range(2):
                    for jj, sl in ((0, slp), (1, slc)):
                        e.matmul(ps[4][kv * 64:(kv + 1) * 64, :], lhsT=vp[:, sl, kv * 64:(kv + 1) * 64], rhs=E_v[iE][:, kv, jj, :],
                                 start=(jj == 0), stop=(jj == 1))
                for kv in range(2):
                    for jj in range(2):
                        ins = e.matmul(ps[5][kv * 64:(kv + 1) * 64, :], lhsT=ones_bf[:], rhs=E_v[iE][:, kv, jj, :],
                                       start=(jj == 0), stop=(jj == 1))
                return ins
            P.op("pe", mm, reads=[B_vp[slp], B_vp[slc], BE, B_misc_main], writes=[B_ps[4], B_ps[5]])
            BA = B_att[f"attn{b % 2}"]
            P.op("dve", lambda e: e.tensor_tensor(out=den[:], in0=ps[5][:], in1=sinkb[:].rearrange("p c q -> p (c q)"), op=ALU.add),
                 reads=[B_ps[5], B_sink], writes=[B_den])
            P.op("act", lambda e: e.activation(out=den[:], in_=den[:], func=AF.Ln), reads=[B_den], writes=[B_den])
            P.op("act", lambda e: e.activation(out=den[:], in_=den[:], func=AF.Exp, scale=-1.0), reads=[B_den], writes=[B_den])
            P.op("dve", lambda e: e.tensor_tensor(out=attn_v[b % 2].rearrange("p c q -> p (c q)"), in0=ps[4][:], in1=den[:], op=ALU.mult),
                 reads=[B_ps[4], B_den], writes=[BA])
            a0 = 0 if first else 8

            def mmp(e):
                for g in range(4):
                    e.matmul(ps[6][:, g * 128:(g + 1) * 128], lhsT=vp[:, slp, 128 + g * 128: 256 + g * 128], rhs=amat[:, a0 + g, :],
                             start=True, stop=False)
                    ins = e.matmul(ps[6][:, g * 128:(g + 1) * 128], lhsT=vp[:, slc, 128 + g * 128: 256 + g * 128], rhs=amat[:, a0 + 4 + g, :],
                                   start=False, stop=True)
                return ins
            P.op("pe", mmp, reads=[B_vp[slp], B_vp[slc], B_const], writes=[B_ps[6]])
            P.op("act", lambda e: e.activation(out=pooled_v.rearrange("p g n -> p (g n)"), in_=ps[6][:], func=AF.Copy),
                 reads=[B_ps[6]], writes=[B_att["pooled"]])

        def mix_out(b, gb, par):
            BA = B_att[f"attn{b % 2}"]

            def mmx(e):
                for g in range(4):
                    ins = e.matmul(ps[6][:, g * 128:(g + 1) * 128], lhsT=wmix[:, g, :], rhs=pooled_v[:, g, :], start=True, stop=True)
                return ins
            P.op("pe", mmx, reads=[B_att["pooled"], B_wmix], writes=[B_ps[6]])
            P.op("act", lambda e: e.activation(out=mixT_v.rearrange("p g n -> p (g n)"), in_=ps[6][:], func=AF.Copy),
                 reads=[B_ps[6]], writes=[B_att["mixT"]])
            for h in range(2):
                def mm(e, h=h):
                    for c in range(4):
                        e.matmul(ps[2 + h][:], lhsT=attn_v[b % 2][:, c, :], rhs=woa_v[:, c, h * 512:(h + 1) * 512],
                                 start=(c == 0), stop=False)
                    for g in range(4):
                        ins = e.matmul(ps[2 + h][:], lhsT=mixT_v[:, g, :], rhs=wop_v[:, g, h * 512:(h + 1) * 512],
                                       start=False, stop=(g == 3))
                    return ins
                P.op("pe", mm, reads=[BA, B_att["mixT"], B_woa, B_wop], writes=[B_ps[2 + h]])
            post_ln_evac(b, (2, 3), par)

        def cast_dma(dst, src, sem, bufs):
            P.op("pool", lambda e, dst=dst, src=src: e.dma_start(out=dst, in_=src), writes=bufs, dma=sem)

        def issue_front_casts():
            for j in range(NGRP):
                for (dst, src) in cast_fam["gu0"][2 * j:2 * j + 2]:
                    cast_dma(dst, src, s_cg0[j], [B_scr_gu0[j]])
            for (dst, src) in cast_fam["win"]:
                P.op("pool", lambda e, dst=dst, src=src: e.dma_start(out=dst, in_=src), dma=s_cwin)
            B_scr_win.last_w = (s_cwin, P.dcnt[s_cwin])
            B_wmix.last_w = (s_cwin, P.dcnt[s_cwin])

        bg_casts = [(d_, s_, s_cg1, B_scr_gu1) for (d_, s_) in cast_fam["gu1"]] + \
                   [(d_, s_, s_cwd2, B_scr_wd2) for (d_, s_) in cast_fam["wd2"]]

        def bg_cast_step(n=2):
            for _ in range(n):
                if not bg_casts:
                    return
                d_, s_, sem, buf = bg_casts.pop(0)
                P.op("pool", lambda e, d_=d_, s_=s_: e.dma_start(out=d_, in_=s_), dma=sem)
                buf.last_w = (sem, P.dcnt[sem])

        tiles = [[0]] + [[1 + 4 * t + i for i in range(4)] for t in range(8)]
        NT = len(tiles)
        order = [("A", 0)]
        for t in range(NT):
            if t + 1 < NT:
                order.append(("A", t + 1))
            order.append(("L", t))
            if t > 0:
                order.append(("C", t))
        for ph, t in order:
            if ph == "A":
                gu_reqs.extend([(0, j) for j in range(NGRP)])
            elif ph == "C":
                gu_reqs.extend([(1, j) for j in range(NGRP)])
        m1_state = {}
        m1_gens = {}

        def m1_start(t):
            gbs = tiles[t]
            m1_gens[t] = [spawn(m1_tile_gen(len(gbs), gbs[0] * 128))]
            m1_state[t] = "n2"

        def m1_finish(t):
            run_until_done(m1_gens[t])
            transpose_evac(len(tiles[t]), 0, hTa, B_hTa)
            m1_state[t] = "done"

        tails = {"ln": []}

        for pi, (ph, t) in enumerate(order):
            gbs = tiles[t]
            nb = len(gbs)
            par = t % 2
            if ph == "A":
                r0 = gbs[0] * 128
                P.op("pool", lambda e, nb=nb, r0=r0, par=par: e.dma_start(
                    out=xt[par][:, 0:nb, :], in_=x_c[r0:r0 + nb * 128, :].rearrange("(b p) d -> p b d", p=128)),
                    writes=B_x[par][0:nb], dma=s_x[par])
                if m1_state.get(t) is None:
                    run_until_done(tails["ln"])
                    m1_start(t)
                    if t == 0:
                        tick()
                        issue_front_casts()
                if m1_state[t] != "done":
                    m1_finish(t)
                gateup(0, nb, hTa, B_hTa)
                run_until_done(tails["ln"])
                tl = t - 1
                m2 = []
                if tl >= 0:
                    m2 = [spawn(modulate_gen(b, xt[tl % 2][:, b, :], B_x[tl % 2][b])) for b in range(len(tiles[tl]))]
                tails["ln"] = down(nb, 0, False, 0, par)
                if tl >= 0:
                    run_until_done(m2)
                    transpose_evac(len(tiles[tl]), 1, hTa, B_hTa)
                    m1_state[("m2", tl)] = True
            elif ph == "L":
                halo = (t == 0)
                load_mixer_w()
                if not m1_state.get(("m2", t)):
                    m2 = [spawn(modulate_gen(b, xt[par][:, b, :], B_x[par][b])) for b in range(nb)]
                    run_until_done(m2)
                    transpose_evac(nb, 1, hTa, B_hTa)
                w_in_proj(nb, gbs, not halo, hTa, B_hTa)
                if halo:
                    continue
                mix_scores(0, gbs[0])
                mix_scores(1, gbs[1])
                run_until_done(tails["ln"])
                gens = []
                for b in range(nb):
                    mix_pv(b, gbs[b])
                    if b + 2 < nb:
                        mix_scores(b + 2, gbs[b + 2])
                    mix_out(b, gbs[b], par)
                    gens.append(spawn(post_ln_gen(b, 1, False, 0, par, True)))
                    tick()
                    tick()
                    tick()
                run_until_done(gens)
                transpose_evac(nb, 2, hTb, B_hTb)
            else:
                nxt = [tt for (pp, tt) in order[pi + 1:] if pp == "A"]
                hook = None
                if nxt and m1_state.get(nxt[0]) is None:
                    tn = nxt[0]
                    m1_start(tn)

                    def hook(tn=tn):
                        if m1_state[tn] == "n2" and not any(g in live for g in m1_gens[tn]):
                            transpose_evac(len(tiles[tn]), 0, hTa, B_hTa)
                            m1_state[tn] = "done"
                gateup(1, nb, hTb, B_hTb, hook)
                if nxt and m1_state[nxt[0]] != "done":
                    m1_finish(nxt[0])
                tails["ln"] = down(nb, 2, True, (gbs[0] - 1) * 128, par)
        drain()

        P.op("pool", lambda e: e.nop(), waits=out_events)
        with nc.Block() as blk:
            P.flush(blk)
    return nc


_NC_CACHE = {}


def _consts():
    ident = np.eye(128, dtype=np.float32)
    j = np.arange(128)[:, None]
    i = np.arange(128)[None, :]
    m_cur = np.where(j <= i, 0.0, NEG).astype(np.float32)
    m_prev = np.where(j > i, 0.0, NEG).astype(np.float32)
    m_none = np.full((128, 128), NEG, np.float32)

    def tile4(m):
        return m
    masks_gen = np.stack([tile4(m_cur), tile4(m_prev), tile4(m_prev)], axis=1)
    masks_first = np.stack([tile4(m_cur), tile4(m_prev), tile4(m_none)], axis=1)
    a_gen = np.zeros((2, 4, 128, 128), np.float32)
    a_first = np.zeros((2, 4, 128, 128), np.float32)
    for g, w in enumerate(POOL_WINDOWS):
        for t in range(128):
            for tpp in range(t - w + 1, t + 1):
                if tpp >= 0:
                    a_gen[1, g, tpp, t] += 1.0 / w
                else:
                    a_gen[0, g, tpp + 128, t] += 1.0 / w
            a_gen[1, g, t, t] -= 1.0
            cnt = min(t + 1, w)
            for tpp in range(max(t - w + 1, 0), t + 1):
                a_first[1, g, tpp, t] += 1.0 / cnt
            a_first[1, g, t, t] -= 1.0
    return ident, masks_gen, masks_first, a_gen, a_first


def kernel(x, c, w_ada, b_ada, ffn1_w_gate, ffn1_w_up, ffn1_w_down, ln1_g, ln1_b,
           w_in, attn_sinks, pool_w_mix, pool_scale, w_out, ln2_g, ln2_b,
           ffn2_w_gate, ffn2_w_up, ffn2_w_down, ln3_g, ln3_b):
    f32 = lambda a: np.ascontiguousarray(np.asarray(a, dtype=np.float32))
    x = f32(x)
    c = f32(c)
    if "nc" not in _NC_CACHE:
        _NC_CACHE["nc"] = build_program()
    nc = _NC_CACHE["nc"]
    ident, masks_gen, masks_first, a_gen, a_first = _consts()
    bf = ml_dtypes.bfloat16
    shared = {
        "w_ada": f32(w_ada[0]), "b_ada": f32(b_ada[0]).reshape(1, -1),
        "ffn1_w_gate": f32(ffn1_w_gate[0]), "ffn1_w_up": f32(ffn1_w_up[0]), "ffn1_w_down": f32(ffn1_w_down[0]),
        "ffn2_w_gate": f32(ffn2_w_gate[0]), "ffn2_w_up": f32(ffn2_w_up[0]), "ffn2_w_down": f32(ffn2_w_down[0]),
        "ln1_g": f32(ln1_g[0]).reshape(1, -1), "ln1_b": f32(ln1_b[0]).reshape(1, -1),
        "ln2_g": f32(ln2_g[0]).reshape(1, -1), "ln2_b": f32(ln2_b[0]).reshape(1, -1),
        "ln3_g": f32(ln3_g[0]).reshape(1, -1), "ln3_b": f32(ln3_b[0]).reshape(1, -1),
        "w_in": f32(w_in[0]), "w_out": f32(w_out[0]),
        "sinks_b": np.ascontiguousarray(np.broadcast_to(f32(attn_sinks[0]).reshape(1, 8), (128, 8))),
        "pool_w_mix": f32(pool_w_mix[0]),
        "pool_scale_p": np.ascontiguousarray(f32(pool_scale[0]).reshape(4, 128).T),
        "ident_bf": ident.astype(bf), "ident_f": ident,
    }
    in_maps = []
    for core in range(8):
        b, half = core // 2, core % 2
        own = x[b, half * TOK_OWN:(half + 1) * TOK_OWN]
        halo = x[b, TOK_OWN - 128:TOK_OWN] if half == 1 else x[b, 0:128]
        m = dict(shared)
        m["x_c"] = np.ascontiguousarray(np.concatenate([halo, own], axis=0))
        m["c_c"] = np.ascontiguousarray(c[b].reshape(8, 128).T)
        m["masks"] = (masks_first if half == 0 else masks_gen).astype(bf)
        af = a_first if half == 0 else a_gen
        am = np.concatenate([af.reshape(8, 128, 128), a_gen.reshape(8, 128, 128)], axis=0)
        m["amat"] = np.ascontiguousarray(am.transpose(1, 0, 2)).astype(bf)
        in_maps.append(m)
    if _NC_CACHE.get("maps_only"):
        return in_maps
    res = run_bass_kernel_spmd(nc, in_maps, core_ids=list(range(8)))
    out = np.empty((4, SEQ, D), np.float32)
    for core in range(8):
        b, half = core // 2, core % 2
        out[b, half * TOK_OWN:(half + 1) * TOK_OWN] = np.asarray(res.results[core]["out_c"], dtype=np.float32)
    return out
```

```python
import numpy as np
import ml_dtypes
from contextlib import ExitStack
import concourse.bass as bass
import concourse.mybir as mybir
from concourse.bass_utils import run_bass_kernel_spmd

F32 = mybir.dt.float32
BF16 = mybir.dt.bfloat16
ALU = mybir.AluOpType
AF = mybir.ActivationFunctionType

D = 1024
FF = 2816
NFC = 22
NGRP = 11
SEQ = 8192
NB_OWN = 32
TOK_OWN = 4096
TOK_IN = TOK_OWN + 128
ALPHA = 2.0 ** 0.25
LN_EPS = 1e-5
EPS_POST = LN_EPS / (ALPHA * ALPHA)
NEG = -30000.0
POOL_WINDOWS = (2, 4, 8, 16)
ENG = ("pe", "act", "dve", "pool", "sp")


class Buf:
    def __init__(self, name, overlaps=()):
        self.name = name
        self.last_w = None
        self.readers = {}
        self.overlaps = list(overlaps)


class Prog:
    def __init__(self, nc, es):
        self.nc = nc
        self.es = es
        self.sems = {}
        for e in ENG:
            self.sems[e] = es.enter_context(nc.semaphore("s_" + e))
        self.cnt = {e: 0 for e in ENG}
        self.dcnt = {}
        self.q = {e: [] for e in ENG}
        self.waited = {e: {} for e in ENG}

    def new_sem(self, name):
        self.sems[name] = self.es.enter_context(self.nc.semaphore(name))
        self.dcnt[name] = 0
        return name

    def op(self, eng, fn, reads=(), writes=(), dma=None, waits=()):
        need = {}

        def add(ev):
            if ev is None:
                return
            k, v = ev
            if need.get(k, 0) < v:
                need[k] = v

        for b in reads:
            add(b.last_w)
        for b in writes:
            add(b.last_w)
            for k, v in b.readers.items():
                add((k, v))
            for o in b.overlaps:
                add(o.last_w)
                for k, v in o.readers.items():
                    add((k, v))
        for ev in waits:
            add(ev)
        wl = []
        for k, v in need.items():
            if k == eng and eng == "pe":
                continue
            if self.waited[eng].get(k, 0) >= v:
                continue
            self.waited[eng][k] = v
            wl.append((k, v))
        if dma is not None:
            self.dcnt[dma] += 16
            ev = (dma, self.dcnt[dma])
        else:
            self.cnt[eng] += 1
            ev = (eng, self.cnt[eng])
        self.q[eng].append((wl, fn, ev))
        for b in reads:
            if b.readers.get(ev[0], 0) < ev[1]:
                b.readers[ev[0]] = ev[1]
        for b in writes:
            b.last_w = ev
            b.readers = {}
        return ev

    def flush(self, block):
        qs = self.q
        self.q = {e: [] for e in ENG}
        sems = self.sems

        def mk(eng):
            def body(e):
                for wl, fn, ev in qs[eng]:
                    for k, v in wl:
                        e.wait_ge(sems[k], v)
                    ins = fn(e)
                    ins.then_inc(sems[ev[0]], 1 if ev[0] in ENG else 16)
            return body

        block.tensor(mk("pe"))
        block.scalar(mk("act"))
        block.vector(mk("dve"))
        block.gpsimd(mk("pool"))
        block.sync(mk("sp"))


def build_program():
    nc = bass.Bass("TRN2", target_bir_lowering=False)

    def din(name, shape, dt=F32):
        return nc.dram_tensor(name, list(shape), dt, kind="ExternalInput").ap()

    x_c = din("x_c", [TOK_IN, D])
    c_c = din("c_c", [128, 8])
    w_ada = din("w_ada", [D, 9 * D])
    b_ada = din("b_ada", [1, 9 * D])
    wg = [din("ffn1_w_gate", [D, FF]), din("ffn2_w_gate", [D, FF])]
    wu = [din("ffn1_w_up", [D, FF]), din("ffn2_w_up", [D, FF])]
    wdn = [din("ffn1_w_down", [FF, D]), din("ffn2_w_down", [FF, D])]
    lng_d = [din("ln1_g", [1, D]), din("ln2_g", [1, D]), din("ln3_g", [1, D])]
    lnb_d = [din("ln1_b", [1, D]), din("ln2_b", [1, D]), din("ln3_b", [1, D])]
    w_in = din("w_in", [D, 1280])
    sinks_d = din("sinks_b", [128, 8])
    wmix_d = din("pool_w_mix", [4, 128, 128])
    pscale_d = din("pool_scale_p", [128, 4])
    w_out = din("w_out", [D, D])
    ident_bf_d = din("ident_bf", [128, 128], BF16)
    ident_f_d = din("ident_f", [128, 128])
    masks_d = din("masks", [128, 3, 128], BF16)
    amat_d = din("amat", [128, 16, 128], BF16)
    out_c = nc.dram_tensor("out_c", [TOK_OWN, D], F32, kind="ExternalOutput").ap()

    scr_gu = [nc.dram_tensor(f"scr_gu{f}", [NGRP, 128, 8, 2, 256], BF16).ap() for f in range(2)]
    scr_d = [nc.dram_tensor(f"scr_d{f}", [128, NFC, D], BF16).ap() for f in range(2)]
    scr_win = nc.dram_tensor("scr_win", [128, 8, 1280], BF16).ap()
    scr_woa = nc.dram_tensor("scr_woa", [128, 4, D], BF16).ap()
    scr_wop = nc.dram_tensor("scr_wop", [128, 4, D], BF16).ap()

    with ExitStack() as es:
        P = Prog(nc, es)

        def sb(name, shape, dt=F32, stack=es):
            return stack.enter_context(nc.sbuf_tensor("t_" + name, list(shape), dt))

        ident_bf = sb("ident_bf", [128, 128], BF16)
        ones_bf = sb("ones_bf", [128, 64], BF16)
        masks = sb("masks", [128, 3, 128], BF16)
        amat = sb("amat", [128, 16, 128], BF16)
        wmix = sb("wmix", [128, 4, 128], BF16)
        pscale = sb("pscale", [128, 4])
        sinkexp = sb("sinkexp", [128, 8])
        sinkb = sb("sinkb", [128, 4, 128])
        lng = [sb(f"lng{i}", [128, D]) for i in range(3)]
        lnb = [sb(f"lnb{i}", [128, D]) for i in range(3)]
        gate3 = sb("gate3", [128, D])
        msc = sb("msc", [128, 6, 8])

        B_const = Buf("const")
        B_gates = Buf("gates")
        B_msc = Buf("msc")
        B_sink = Buf("sink")

        s_const = P.new_sem("d_const")

        with ExitStack() as pes:
            ct = sb("ct", [128, 8], F32, pes)
            ident_f = sb("ident_f", [128, 128], F32, pes)
            cs = sb("cs", [128, 8], F32, pes)
            crep = sb("crep", [128, 8, 128], BF16, pes)
            ones_f = sb("ones_f", [1, 128], F32, pes)
            bada = sb("bada", [1, 9 * D], F32, pes)
            modb = sb("modb", [128, 9 * D], F32, pes)
            wada_t = [sb(f"wada{i}", [128, 8, 256], BF16, pes) for i in range(4)]
            dtmp = sb("dtmp", [128, 8, 128], F32, pes)
            gates = [sb(f"gate{i}", [128, D], F32, pes) for i in range(3)]
            stg_in = [sb(f"stg_in{i}", [128, 2, D], F32, pes) for i in range(3)]
            stg_out = [sb(f"stg_out{i}", [128, 2, D], BF16, pes) for i in range(3)]
            B_si = [Buf(f"si{i}") for i in range(3)]
            B_so = [Buf(f"so{i}") for i in range(3)]
            s_si = [P.new_sem(f"d_si{i}") for i in range(3)]
            s_so = [P.new_sem(f"d_so{i}") for i in range(3)]
            modps = [pes.enter_context(nc.psum_tensor(f"modps{i}", [128, 512], F32)) for i in range(2)]
            B_modps = [Buf(f"modps{i}") for i in range(2)]
            B_wada = [Buf(f"wada{i}") for i in range(4)]
            s_wada = [P.new_sem(f"d_wada{i}") for i in range(4)]
            B_modb = Buf("modb")
            B_misc = Buf("misc")
            B_dtmp = Buf("dtmp")

            cast_fam = {"gu0": [], "gu1": [], "win": []}
            for f in range(2):
                cast_ops = cast_fam[f"gu{f}"]
                for j in range(NGRP):
                    for gu, wsrc in enumerate((wg[f], wu[f])):
                        src = wsrc.rearrange("(kc p) (j n) -> j p kc n", p=128, n=256)[j]
                        dst = scr_gu[f][j][:, :, gu, :]
                        cast_ops.append((dst, src))
            cast_fam["wd2"] = []
            for j in range(NGRP):
                cast_fam["wd2"].append((scr_d[1][:, 2 * j:2 * j + 2, :],
                                        wdn[1].rearrange("(fc p) d -> p fc d", p=128)[:, 2 * j:2 * j + 2, :]))
            w_in_v = w_in.rearrange("(kc p) n -> p kc n", p=128)
            cast_ops = cast_fam["win"]
            cast_ops.append((scr_win[:, :, 512:1280], w_in_v[:, :, 512:1280]))
            for kc in range(8):
                for half in range(2):
                    src = w_in[kc * 128:(kc + 1) * 128, half * 256:(half + 1) * 256].rearrange("p (c d) -> p c d", d=64)
                    dst = scr_win[:, kc, 0:512].rearrange("p (c h d) -> p c h d", h=2, d=64)[:, :, half, :]
                    cast_ops.append((dst, src))
            cast_ops.append((wmix[:], wmix_d.rearrange("g c d -> c g d")))
            const_loads = [
                (ident_bf[:], ident_bf_d), (ident_f[:], ident_f_d), (masks[:], masks_d), (amat[:], amat_d),
                (pscale[:], pscale_d), (sinkexp[:], sinks_d), (ct[:], c_c), (bada[:], b_ada),
            ]
            for i in range(3):
                const_loads.append((lng[i][:], lng_d[i].partition_broadcast(128)))
                const_loads.append((lnb[i][:], lnb_d[i].partition_broadcast(128)))
            for dst, src in const_loads:
                P.op("sp", lambda e, dst=dst, src=src: e.dma_start(out=dst, in_=src), dma=s_const)
            EV_CONST = (s_const, P.dcnt[s_const])
            B_const.last_w = EV_CONST

            P.op("dve", lambda e: e.memset(ones_bf[:], 1.0), writes=[B_misc])
            P.op("dve", lambda e: e.memset(ones_f[:], 1.0), writes=[B_misc])
            P.op("act", lambda e: e.activation(out=cs[:], in_=ct[:], func=AF.Silu), reads=[B_const], writes=[B_misc])
            P.op("dve", lambda e: e.tensor_copy(out=crep[:], in_=cs[:].unsqueeze(2).to_broadcast([128, 8, 128])),
                 reads=[B_misc], writes=[B_misc])
            P.op("act", lambda e: e.activation(out=sinkexp[:], in_=sinkexp[:], func=AF.Exp), reads=[B_const], writes=[B_sink])
            for kv in range(2):
                P.op("dve", lambda e, kv=kv: e.tensor_copy(
                    out=sinkb[kv * 64:(kv + 1) * 64, :, :],
                    in_=sinkexp[kv * 64:(kv + 1) * 64, kv * 4:(kv + 1) * 4].unsqueeze(2).to_broadcast([64, 4, 128])),
                    reads=[B_sink], writes=[B_sink])

            w_ada_v = w_ada.rearrange("(kc p) n -> p kc n", p=128)
            for j in range(36):
                s = j % 4
                pm = j % 2
                P.op("pool", lambda e, s=s, j=j: e.dma_start(out=wada_t[s][:], in_=w_ada_v[:, :, j * 256:(j + 1) * 256]),
                     writes=[B_wada[s]], dma=s_wada[s])

                def mm(e, s=s, j=j, pm=pm):
                    for kc in range(8):
                        e.matmul(modps[pm][:, 0:256], lhsT=crep[:, kc, :], rhs=wada_t[s][:, kc, :], start=(kc == 0), stop=False)
                    return e.matmul(modps[pm][:, 0:256], lhsT=ones_f[:], rhs=bada[:, j * 256:(j + 1) * 256], start=False, stop=True)
                P.op("pe", mm, reads=[B_wada[s], B_misc, B_const], writes=[B_modps[pm]])
                P.op("act", lambda e, pm=pm, j=j: e.activation(out=modb[:, j * 256:(j + 1) * 256], in_=modps[pm][:, 0:256], func=AF.Copy),
                     reads=[B_modps[pm]], writes=[B_modb])

            gsc = (0.5 / ALPHA, 1.0 / ALPHA, 0.5 / ALPHA)
            for i in range(3):
                off = (3 * i + 2) * D
                P.op("dve", lambda e, i=i, off=off: e.tensor_scalar_mul(out=gates[i][:], in0=modb[:, off:off + D], scalar1=gsc[i]),
                     reads=[B_modb], writes=[B_gates])
            P.op("dve", lambda e: e.tensor_copy(out=gate3[:], in_=gates[2][:]), reads=[B_gates], writes=[B_gates])
            for i in range(3):
                for k in range(2):
                    off = (3 * i + k) * D
                    P.op("dve", lambda e, off=off: e.tensor_tensor(
                        out=dtmp[:], in0=modb[:, off:off + D].rearrange("p (k n) -> p k n", n=128),
                        in1=ident_f[:].unsqueeze(1).to_broadcast([128, 8, 128]), op=ALU.mult),
                        reads=[B_modb, B_const], writes=[B_dtmp])
                    P.op("dve", lambda e, i=i, k=k: e.tensor_reduce(
                        out=msc[:, 2 * i + k, :], in_=dtmp[:], axis=mybir.AxisListType.X, op=ALU.add),
                        reads=[B_dtmp], writes=[B_msc])
                P.op("dve", lambda e, i=i: e.tensor_scalar_add(out=msc[:, 2 * i + 1, :], in0=msc[:, 2 * i + 1, :], scalar1=1.0),
                     reads=[B_msc], writes=[B_msc])
            fold = []
            for f in range(1):
                for j in range(NGRP):
                    fold.append(([(slice(0, 128), r_, wdn[f][(2 * j + r_) * 128:(2 * j + r_ + 1) * 128, :]) for r_ in range(2)],
                                 2 * f, None, scr_d[f][:, 2 * j:2 * j + 2, :]))
            for cp in range(2):
                srcs = []
                for r_ in range(2):
                    c = 2 * cp + r_
                    srcs.append((slice(0, 64), r_, w_out[c * 64:(c + 1) * 64, :]))
                    srcs.append((slice(64, 128), r_, w_out[256 + c * 64:256 + (c + 1) * 64, :]))
                fold.append((srcs, 1, None, scr_woa[:, 2 * cp:2 * cp + 2, :]))
            for gp in range(2):
                srcs = [(slice(0, 128), r_, w_out[512 + (2 * gp + r_) * 128:512 + (2 * gp + r_ + 1) * 128, :]) for r_ in range(2)]
                fold.append((srcs, 1, (2 * gp, 2 * gp + 1), scr_wop[:, 2 * gp:2 * gp + 2, :]))
            fold_ev = []
            for i, (srcs, gi, pgs, dst) in enumerate(fold):
                k = i % 3
                for sl, r_, src in srcs:
                    P.op("sp", lambda e, k=k, sl=sl, r_=r_, src=src: e.dma_start(out=stg_in[k][sl, r_, :], in_=src),
                         writes=[B_si[k]], dma=s_si[k])
                for r_ in range(2):
                    if pgs is None:
                        P.op("dve", lambda e, k=k, gi=gi, r_=r_: e.tensor_tensor(out=stg_out[k][:, r_, :], in0=stg_in[k][:, r_, :],
                                                                                 in1=gates[gi][:], op=ALU.mult),
                             reads=[B_si[k], B_gates], writes=[B_so[k]])
                    else:
                        P.op("dve", lambda e, k=k, gi=gi, r_=r_, pg=pgs[r_]: e.scalar_tensor_tensor(
                            out=stg_out[k][:, r_, :], in0=stg_in[k][:, r_, :], scalar=pscale[:, pg:pg + 1], in1=gates[gi][:],
                            op0=ALU.mult, op1=ALU.mult),
                            reads=[B_si[k], B_gates, B_const], writes=[B_so[k]])
                ev = P.op("act", lambda e, k=k, dst=dst: e.dma_start(out=dst, in_=stg_out[k][:]), reads=[B_so[k]], dma=s_so[k])
                fold_ev.append(ev)
            P.op("sp", lambda e: e.nop(), waits=fold_ev[-3:])
            with nc.Block() as blk:
                P.flush(blk)

        xt = [sb(f"xt{i}", [128, 4, D]) for i in range(2)]
        rbuf = [sb(f"rbuf{i}", [128, D]) for i in range(4)]
        n2b = [sb(f"n2b{i}", [128, D], BF16) for i in range(4)]
        hTa = sb("hTa", [128, 8, 512], BF16)
        hTb = sb("hTb", [128, 8, 512], BF16)
        sg = [sb(f"sg{i}", [128, 512], BF16) for i in range(2)]
        NRING = 2
        wgu = [sb(f"wgu{i}", [128, 8, 2, 256], BF16) for i in range(NRING)]
        u1 = sb("u1", [128, NFC * D], BF16)
        u2 = sb("u2", [128, NFC * 512], BF16)
        kT = sb("kT", [128, 8, 128], BF16)
        vp = sb("vp", [128, 8, 640], BF16)
        den = sb("den", [128, 512])
        stt = sb("stt", [128, 4, 2, 6])
        mv = sb("mv", [128, 4, 2])
        rs = sb("rs", [128, 4, 2])
        stt2 = sb("stt2", [128, 4, 2, 6])
        mv2 = sb("mv2", [128, 4, 2])
        rs2 = sb("rs2", [128, 4, 2])
        print("sbuf bytes remaining/partition:", nc.sbuf_bytes_remaining)

        ps = [es.enter_context(nc.psum_tensor(f"ps{i}", [128, 512], F32)) for i in range(7)]
        tp = es.enter_context(nc.psum_tensor("tp", [128, 2, 512], BF16))
        B_ps = [Buf(f"ps{i}") for i in range(7)]
        B_tp = Buf("tp")

        wd_v = u1[:].rearrange("p (f n) -> p f n", n=D)
        win_v = u1[:, 0:10240].rearrange("p (k n) -> p k n", n=1280)
        woa_v = u1[:, 10240:14336].rearrange("p (h n) -> p h n", n=D)
        wop_v = u1[:, 14336:18432].rearrange("p (g n) -> p g n", n=D)
        aT_v = u2[:].rearrange("p (f n) -> p f n", n=512)
        qT_v = u2[:, 0:2048].rearrange("p (c n) -> p c n", n=512)
        E_v = [u2[:, 2048 + i * 2048: 4096 + i * 2048].rearrange("p (a b n) -> p a b n", a=2, b=2) for i in range(2)]
        attn_v = [u2[:, 6144 + i * 512: 6656 + i * 512].rearrange("p (c n) -> p c n", n=128) for i in range(2)]
        pooled_v = u2[:, 7168:7680].rearrange("p (g n) -> p g n", n=128)
        mixT_v = u2[:, 7680:8192].rearrange("p (g n) -> p g n", n=128)

        B_wd = [Buf(f"wd{j}") for j in range(NGRP)]
        B_win = Buf("win", overlaps=B_wd[0:5])
        B_woa = Buf("woa", overlaps=B_wd[5:7])
        B_wop = Buf("wop", overlaps=B_wd[7:9])
        for j in range(0, 5):
            B_wd[j].overlaps.append(B_win)
        for j in range(5, 7):
            B_wd[j].overlaps.append(B_woa)
        for j in range(7, 9):
            B_wd[j].overlaps.append(B_wop)
        s_wd = [P.new_sem(f"d_wd{j}") for j in range(NGRP)]
        B_aT = [Buf(f"aT{f}") for f in range(NFC)]
        B_att = {n: Buf(n, overlaps=B_aT) for n in ("qT", "E0", "E1", "attn0", "attn1", "pooled", "mixT")}
        for b in B_aT:
            b.overlaps.extend(B_att.values())
        B_wgu = [Buf(f"wgu{i}") for i in range(NRING)]
        s_wgu = [P.new_sem(f"d_wgu{i}") for i in range(NRING)]
        B_x = [[Buf(f"x{p}_{b}") for b in range(4)] for p in range(2)]
        s_x = [P.new_sem(f"d_x{p}") for p in range(2)]
        B_r = [Buf(f"r{i}") for i in range(4)]
        s_r = [P.new_sem(f"d_r{i}") for i in range(4)]
        B_n2 = [Buf(f"n2b{i}") for i in range(4)]
        B_hTa = [Buf(f"hTa{b}") for b in range(4)]
        B_hTb = [Buf(f"hTb{b}") for b in range(4)]
        B_sg = [Buf(f"sg{i}") for i in range(2)]
        B_kT = [Buf(f"kT{i}") for i in range(8)]
        B_vp = [Buf(f"vp{i}") for i in range(8)]
        B_den = Buf("den")
        B_st = [Buf(f"st{b}") for b in range(4)]
        B_st2 = [Buf(f"st2{b}") for b in range(4)]
        B_scr = Buf("scr")
        B_scr_gu0 = [Buf(f"scr_gu0_{j}") for j in range(NGRP)]
        s_cg0 = [P.new_sem(f"d_cg0_{j}") for j in range(NGRP)]
        B_scr_gu1 = Buf("scr_gu1")
        s_cg1 = P.new_sem("d_cg1")
        B_scr_wd2 = Buf("scr_wd2")
        s_cwd2 = P.new_sem("d_cwd2")
        B_scr_win = Buf("scr_win")
        s_cwin = P.new_sem("d_cwin")
        B_wmix = Buf("wmix")
        B_misc_main = Buf("misc_main")
        out_events = []

        gu_reqs = []
        gu_state = {"issued": 0, "cons": 0, "slot": 0}
        r_ctr = {"sg": 0}

        def issue_gu(upto):
            while gu_state["issued"] < min(upto, len(gu_reqs)):
                i = gu_state["issued"]
                f, j = gu_reqs[i]
                s = i % NRING
                P.op("sp", lambda e, f=f, j=j, s=s: e.dma_start(out=wgu[s][:], in_=scr_gu[f][j]),
                     reads=[B_scr_gu0[j] if f == 0 else B_scr_gu1], writes=[B_wgu[s]], dma=s_wgu[s])
                gu_state["issued"] += 1

        def load_wd_slot(f, j):
            P.op("sp", lambda e, f=f, j=j: e.dma_start(out=wd_v[:, 2 * j:2 * j + 2, :], in_=scr_d[f][:, 2 * j:2 * j + 2, :]),
                 reads=[B_scr if f == 0 else B_scr_wd2], writes=[B_wd[j]], dma=s_wd[j])

        def load_mixer_w():
            P.op("sp", lambda e: e.dma_start(out=win_v, in_=scr_win), reads=[B_scr_win], writes=[B_win], dma=s_wd[0])
            P.op("sp", lambda e: e.dma_start(out=woa_v, in_=scr_woa), reads=[B_scr], writes=[B_woa], dma=s_wd[5])
            P.op("sp", lambda e: e.dma_start(out=wop_v, in_=scr_wop), reads=[B_scr], writes=[B_wop], dma=s_wd[7])

        def stats_gen(src_ap, st_t, mv_t, rs_t, b, B_src, B_s, eps):
            for h in range(2):
                P.op("dve", lambda e, h=h: e.bn_stats(out=st_t[:, b, h, :], in_=src_ap[:, h * 512:(h + 1) * 512]),
                     reads=[B_src], writes=[B_s])
            P.op("dve", lambda e: e.bn_aggr(out=mv_t[:, b, :], in_=st_t[:, b, :, :].rearrange("p a s -> p (a s)")), reads=[B_s], writes=[B_s])
            P.op("dve", lambda e: e.tensor_scalar_add(out=rs_t[:, b, 0:1], in0=mv_t[:, b, 1:2], scalar1=eps),
                 reads=[B_s], writes=[B_s])
            yield
            P.op("act", lambda e: e.activation(out=rs_t[:, b, 0:1], in_=rs_t[:, b, 0:1], func=AF.Ln), reads=[B_s], writes=[B_s])
            P.op("act", lambda e: e.activation(out=rs_t[:, b, 0:1], in_=rs_t[:, b, 0:1], func=AF.Exp, scale=-0.5),
                 reads=[B_s], writes=[B_s])
            yield
            P.op("dve", lambda e: e.tensor_scalar(out=rs_t[:, b, 1:2], in0=mv_t[:, b, 0:1], scalar1=rs_t[:, b, 0:1],
                                                  scalar2=-1.0, op0=ALU.mult, op1=ALU.mult), reads=[B_s], writes=[B_s])

        def modulate_gen(b, src_ap, B_src):
            yield from stats_gen(src_ap, stt2, mv2, rs2, b, B_src, B_st2[b], LN_EPS)
            P.op("act", lambda e: e.activation(out=n2b[b][:], in_=src_ap, func=AF.Identity,
                                               scale=rs2[:, b, 0:1], bias=rs2[:, b, 1:2]),
                 reads=[B_src, B_st2[b]], writes=[B_n2[b]])

        def m1_tile_gen(nb, row0):
            for b in range(nb):
                P.op("pool", lambda e, b=b: e.dma_start(out=rbuf[b][:], in_=x_c[row0 + b * 128:row0 + (b + 1) * 128, :]),
                     writes=[B_r[b]], dma=s_r[b])
            yield
            for b in range(nb):
                for h in range(2):
                    P.op("dve", lambda e, b=b, h=h: e.bn_stats(out=stt2[:, b, h, :], in_=rbuf[b][:, h * 512:(h + 1) * 512]),
                         reads=[B_r[b]], writes=[B_st2[b]])
                P.op("dve", lambda e, b=b: e.bn_aggr(out=mv2[:, b, :], in_=stt2[:, b, :, :].rearrange("p a s -> p (a s)")),
                     reads=[B_st2[b]], writes=[B_st2[b]])
                yield
            P.op("dve", lambda e: e.tensor_scalar_add(out=rs2[:, 0:nb, 0:1], in0=mv2[:, 0:nb, 1:2], scalar1=LN_EPS),
                 reads=B_st2[0:nb], writes=B_st2[0:nb])
            yield
            P.op("act", lambda e: e.activation(out=rs2[:, 0:nb, 0:1], in_=rs2[:, 0:nb, 0:1], func=AF.Ln), reads=B_st2[0:nb], writes=B_st2[0:nb])
            P.op("act", lambda e: e.activation(out=rs2[:, 0:nb, 0:1], in_=rs2[:, 0:nb, 0:1], func=AF.Exp, scale=-0.5),
                 reads=B_st2[0:nb], writes=B_st2[0:nb])
            yield
            P.op("dve", lambda e: e.scalar_tensor_tensor(out=rs2[:, 0:nb, 1:2], in0=mv2[:, 0:nb, 0:1], scalar=-1.0, in1=rs2[:, 0:nb, 0:1],
                                                         op0=ALU.mult, op1=ALU.mult), reads=B_st2[0:nb], writes=B_st2[0:nb])
            yield
            for b in range(nb):
                P.op("act", lambda e, b=b: e.activation(out=n2b[b][:], in_=rbuf[b][:], func=AF.Identity,
                                                        scale=rs2[:, b, 0:1], bias=rs2[:, b, 1:2]),
                     reads=[B_r[b], B_st2[b]], writes=[B_n2[b]])
                yield

        def transpose_evac(nb, mi, hdst, B_hdst):
            T = nb * 128
            for r in range(4):
                def tr(e, r=r):
                    for j in range(2):
                        kc = 2 * r + j
                        for b in range(nb):
                            ins = e.transpose(out=tp[:, j, b * 128:(b + 1) * 128], in_=n2b[b][:, kc * 128:(kc + 1) * 128],
                                              identity=ident_bf[:])
                    return ins
                P.op("pe", tr, reads=B_n2[0:nb] + [B_const], writes=[B_tp])
                for j in range(2):
                    kc = 2 * r + j
                    P.op("act", lambda e, j=j, kc=kc: e.activation(out=hdst[:, kc, 0:T], in_=tp[:, j, 0:T], func=AF.Identity,
                                                                   scale=msc[:, 2 * mi + 1, kc:kc + 1], bias=msc[:, 2 * mi, kc:kc + 1]),
                         reads=[B_tp, B_msc], writes=B_hdst[0:nb])

        def post_ln_evac(b, yb, par, gated=False):
            r = rbuf[b]
            for h in range(2):
                hs = slice(h * 512, (h + 1) * 512)
                if gated:
                    P.op("dve", lambda e, h=h, hs=hs: e.tensor_tensor(out=r[:, hs], in0=ps[yb[h]][:], in1=gate3[:, hs], op=ALU.mult),
                         reads=[B_ps[yb[h]], B_gates], writes=[B_r[b]])
                    P.op("dve", lambda e, hs=hs: e.tensor_tensor(out=r[:, hs], in0=r[:, hs], in1=xt[par][:, b, hs], op=ALU.add),
                         reads=[B_r[b], B_x[par][b]], writes=[B_r[b]])
                else:
                    P.op("dve", lambda e, h=h, hs=hs: e.tensor_tensor(out=r[:, hs], in0=ps[yb[h]][:], in1=xt[par][:, b, hs], op=ALU.add),
                         reads=[B_ps[yb[h]], B_x[par][b]], writes=[B_r[b]])

        def post_ln_gen(b, li, final, row0, par, then_mod):
            r = rbuf[b]
            yield from stats_gen(r, stt, mv, rs, b, B_r[b], B_st[b], EPS_POST)
            P.op("act", lambda e: e.activation(out=r[:], in_=r[:], func=AF.Identity, scale=rs[:, b, 0:1], bias=rs[:, b, 1:2]),
                 reads=[B_r[b], B_st[b]], writes=[B_r[b]])
            yield
            P.op("dve", lambda e: e.tensor_tensor(out=r[:], in0=r[:], in1=lng[li][:], op=ALU.mult),
                 reads=[B_r[b], B_const], writes=[B_r[b]])
            if final:
                P.op("pool", lambda e: e.tensor_tensor(out=r[:], in0=r[:], in1=lnb[li][:], op=ALU.add),
                     reads=[B_r[b], B_const], writes=[B_r[b]])
                ev = P.op("pool", lambda e: e.dma_start(out=out_c[row0 + b * 128: row0 + (b + 1) * 128, :], in_=r[:]),
                          reads=[B_r[b]], dma=s_r[b])
                out_events.append(ev)
                return
            P.op("pool", lambda e: e.tensor_tensor(out=xt[par][:, b, :], in0=r[:], in1=lnb[li][:], op=ALU.add),
                 reads=[B_r[b], B_const], writes=[B_x[par][b]])
            yield
            if then_mod:
                yield from modulate_gen(b, xt[par][:, b, :], B_x[par][b])

        live = []

        def tick():
            for g in list(live):
                try:
                    next(g)
                except StopIteration:
                    live.remove(g)

        def run_until_done(gens):
            while any(g in live for g in gens):
                tick()

        def drain():
            while live:
                tick()

        def spawn(g):
            live.append(g)
            return g

        def gateup(f, nb, hsrc, B_hsrc, hook=None):
            T = nb * 128
            for fc in range(NFC):
                s2 = fc % 2
                if s2 == 0:
                    gi = gu_state["cons"]
                    gu_state["cons"] += 1
                    issue_gu(gi + NRING)
                    gu_state["slot"] = gi % NRING
                slot = gu_state["slot"]
                pg, pu = fc % 2, 2 + fc % 2

                def mm(e, slot=slot, s2=s2, pg=pg, pu=pu):
                    for gu, pb in ((0, pg), (1, pu)):
                        for kc in range(8):
                            ins = e.matmul(ps[pb][:, 0:T], lhsT=wgu[slot][:, kc, gu, s2 * 128:(s2 + 1) * 128],
                                           rhs=hsrc[:, kc, 0:T], start=(kc == 0), stop=(kc == 7))
                    return ins
                P.op("pe", mm, reads=[B_wgu[slot]] + B_hsrc[0:nb], writes=[B_ps[pg], B_ps[pu]])
                isg = r_ctr["sg"] % 2
                r_ctr["sg"] += 1
                P.op("act", lambda e, pg=pg, isg=isg: e.activation(out=sg[isg][:, 0:T], in_=ps[pg][:, 0:T], func=AF.Silu),
                     reads=[B_ps[pg]], writes=[B_sg[isg]])
                P.op("dve", lambda e, pu=pu, isg=isg, fc=fc: e.tensor_tensor(out=aT_v[:, fc, 0:T], in0=ps[pu][:, 0:T],
                                                                             in1=sg[isg][:, 0:T], op=ALU.mult),
                     reads=[B_ps[pu], B_sg[isg]], writes=[B_aT[fc]])
                tick()
                if fc % 2 == 1:
                    load_wd_slot(f, fc // 2)
                if nb == 4:
                    bg_cast_step(2)
                if hook is not None:
                    hook()

        def down(nb, li, final, row0, par):
            gens = []
            for b in range(nb):
                for h in range(2):
                    def mm(e, b=b, h=h):
                        for fc in range(NFC):
                            ins = e.matmul(ps[4 + h][:], lhsT=aT_v[:, fc, b * 128:(b + 1) * 128],
                                           rhs=wd_v[:, fc, h * 512:(h + 1) * 512], start=(fc == 0), stop=(fc == NFC - 1))
                        return ins
                    P.op("pe", mm, reads=B_aT + B_wd, writes=[B_ps[4 + h]])
                post_ln_evac(b, (4, 5), par, gated=final)
                gens.append(spawn(post_ln_gen(b, li, final, row0, par, False)))
                tick()
                tick()
            return gens

        def w_in_proj(nb, gbs, with_q, hsrc, B_hsrc):
            T = nb * 128
            chunks = ([0, 1, 2, 3] if with_q else []) + [4]
            for ci, c in enumerate(chunks):
                pb = ci % 2

                def mm(e, c=c, pb=pb):
                    for kc in range(8):
                        ins = e.matmul(ps[pb][:, 0:T], lhsT=win_v[:, kc, c * 128:(c + 1) * 128], rhs=hsrc[:, kc, 0:T],
                                       start=(kc == 0), stop=(kc == 7))
                    return ins
                P.op("pe", mm, reads=[B_win] + B_hsrc[0:nb], writes=[B_ps[pb]])
                if c < 4:
                    P.op("act", lambda e, c=c, pb=pb: e.activation(out=qT_v[:, c, 0:T], in_=ps[pb][:, 0:T], func=AF.Copy),
                         reads=[B_ps[pb]], writes=[B_att["qT"]])
                else:
                    for b in range(nb):
                        sl = gbs[b] % 8
                        P.op("act", lambda e, b=b, sl=sl, pb=pb: e.activation(out=kT[:, sl, :], in_=ps[pb][:, b * 128:(b + 1) * 128], func=AF.Copy),
                             reads=[B_ps[pb]], writes=[B_kT[sl]])
                tick()
            for b in range(nb):
                sl = gbs[b] % 8
                pv_, pp_ = (2, 3) if b % 2 == 0 else (6, 4)

                def mm(e, b=b, pv_=pv_, pp_=pp_):
                    for kc in range(8):
                        e.matmul(ps[pv_][:, 0:128], lhsT=hsrc[:, kc, b * 128:(b + 1) * 128], rhs=win_v[:, kc, 640:768],
                                 start=(kc == 0), stop=(kc == 7))
                        ins = e.matmul(ps[pp_][:], lhsT=hsrc[:, kc, b * 128:(b + 1) * 128], rhs=win_v[:, kc, 768:1280],
                                       start=(kc == 0), stop=(kc == 7))
                    return ins
                P.op("pe", mm, reads=[B_win, B_hsrc[b]], writes=[B_ps[pv_], B_ps[pp_]])
                P.op("dve", lambda e, sl=sl, pv_=pv_: e.tensor_copy(out=vp[:, sl, 0:128], in_=ps[pv_][:, 0:128]),
                     reads=[B_ps[pv_]], writes=[B_vp[sl]])
                P.op("act", lambda e, sl=sl, pp_=pp_: e.activation(out=vp[:, sl, 128:640], in_=ps[pp_][:], func=AF.Copy),
                     reads=[B_ps[pp_]], writes=[B_vp[sl]])
                tick()

        stb = {(0, 0): 0, (0, 1): 1, (1, 0): 2, (1, 1): 3}

        def mix_scores(b, gb):
            slc, slp = gb % 8, (gb - 1) % 8
            first = (gb == 1)
            iE = b % 2
            BE = B_att[f"E{iE}"]
            for jj, sl in ((0, slp), (1, slc)):
                mi = (2 if first else 1) if jj == 0 else 0
                for kv in range(2):
                    pb = stb[(kv, jj)]

                    def mm(e, kv=kv, sl=sl, pb=pb, mi=mi):
                        e.matmul(ps[pb][:].rearrange("p (c q) -> p c q", c=4), lhsT=kT[kv * 64:(kv + 1) * 64, sl, :],
                                 rhs=qT_v[kv * 64:(kv + 1) * 64, :, b * 128:(b + 1) * 128], start=True, stop=False)
                        return e.matmul(ps[pb][:].rearrange("p (c q) -> p c q", c=4), lhsT=ident_bf[:],
                                        rhs=masks[:, mi, :].unsqueeze(1).to_broadcast([128, 4, 128]), start=False, stop=True)
                    P.op("pe", mm, reads=[B_kT[sl], B_att["qT"], B_const], writes=[B_ps[pb]])
                    P.op("act", lambda e, kv=kv, jj=jj, pb=pb: e.activation(out=E_v[iE][:, kv, jj, :], in_=ps[pb][:], func=AF.Exp, scale=0.125),
                         reads=[B_ps[pb]], writes=[BE])

        def mix_pv(b, gb):
            slc, slp = gb % 8, (gb - 1) % 8
            first = (gb == 1)
            iE = b % 2
            BE = B_att[f"E{iE}"]

            def mm(e):
                for kv in range(2):
                    for jj, sl in ((0, slp), (1, slc)):
                        e.matmul(ps[4][kv * 64:(kv + 1) * 64, :], lhsT=vp[:, sl, kv * 64:(kv + 1) * 64], rhs=E_v[iE][:, kv, jj, :],
                                 start=(jj == 0), stop=(jj == 1))
                for kv in range(2):
                    for jj in range(2):
                        ins = e.matmul(ps[5][kv * 64:(kv + 1) * 64, :], lhsT=ones_bf[:], rhs=E_v[iE][:, kv, jj, :],
                                       start=(jj == 0), stop=(jj == 1))
                return ins
            P.op("pe", mm, reads=[B_vp[slp], B_vp[slc], BE, B_misc_main], writes=[B_ps[4], B_ps[5]])
            BA = B_att[f"attn{b % 2}"]
            P.op("dve", lambda e: e.tensor_tensor(out=den[:], in0=ps[5][:], in1=sinkb[:].rearrange("p c q -> p (c q)"), op=ALU.add),
                 reads=[B_ps[5], B_sink], writes=[B_den])
            P.op("act", lambda e: e.activation(out=den[:], in_=den[:], func=AF.Ln), reads=[B_den], writes=[B_den])
            P.op("act", lambda e: e.activation(out=den[:], in_=den[:], func=AF.Exp, scale=-1.0), reads=[B_den], writes=[B_den])
            P.op("dve", lambda e: e.tensor_tensor(out=attn_v[b % 2].rearrange("p c q -> p (c q)"), in0=ps[4][:], in1=den[:], op=ALU.mult),
                 reads=[B_ps[4], B_den], writes=[BA])
            a0 = 0 if first else 8

            def mmp(e):
                for g in range(4):
                    e.matmul(ps[6][:, g * 128:(g + 1) * 128], lhsT=vp[:, slp, 128 + g * 128: 256 + g * 128], rhs=amat[:, a0 + g, :],
                             start=True, stop=False)
                    ins = e.matmul(ps[6][:, g * 128:(g + 1) * 128], lhsT=vp[:, slc, 128 + g * 128: 256 + g * 128], rhs=amat[:, a0 + 4 + g, :],
                                   start=False, stop=True)
                return ins
            P.op("pe", mmp, reads=[B_vp[slp], B_vp[slc], B_const], writes=[B_ps[6]])
            P.op("act", lambda e: e.activation(out=pooled_v.rearrange("p g n -> p (g n)"), in_=ps[6][:], func=AF.Copy),
                 reads=[B_ps[6]], writes=[B_att["pooled"]])

        def mix_out(b, gb, par):
            BA = B_att[f"attn{b % 2}"]

            def mmx(e):
                for g in range(4):
                    ins = e.matmul(ps[6][:, g * 128:(g + 1) * 128], lhsT=wmix[:, g, :], rhs=pooled_v[:, g, :], start=True, stop=True)
                return ins
            P.op("pe", mmx, reads=[B_att["pooled"], B_wmix], writes=[B_ps[6]])
            P.op("act", lambda e: e.activation(out=mixT_v.rearrange("p g n -> p (g n)"), in_=ps[6][:], func=AF.Copy),
                 reads=[B_ps[6]], writes=[B_att["mixT"]])
            for h in range(2):
                def mm(e, h=h):
                    for c in range(4):
                        e.matmul(ps[2 + h][:], lhsT=attn_v[b % 2][:, c, :], rhs=woa_v[:, c, h * 512:(h + 1) * 512],
                                 start=(c == 0), stop=False)
                    for g in range(4):
                        ins = e.matmul(ps[2 + h][:], lhsT=mixT_v[:, g, :], rhs=wop_v[:, g, h * 512:(h + 1) * 512],
                                       start=False, stop=(g == 3))
                    return ins
                P.op("pe", mm, reads=[BA, B_att["mixT"], B_woa, B_wop], writes=[B_ps[2 + h]])
            post_ln_evac(b, (2, 3), par)

        def cast_dma(dst, src, sem, bufs):
            P.op("pool", lambda e, dst=dst, src=src: e.dma_start(out=dst, in_=src), writes=bufs, dma=sem)

        def issue_front_casts():
            for j in range(NGRP):
                for (dst, src) in cast_fam["gu0"][2 * j:2 * j + 2]:
                    cast_dma(dst, src, s_cg0[j], [B_scr_gu0[j]])
            for (dst, src) in cast_fam["win"]:
                P.op("pool", lambda e, dst=dst, src=src: e.dma_start(out=dst, in_=src), dma=s_cwin)
            B_scr_win.last_w = (s_cwin, P.dcnt[s_cwin])
            B_wmix.last_w = (s_cwin, P.dcnt[s_cwin])

        bg_casts = [(d_, s_, s_cg1, B_scr_gu1) for (d_, s_) in cast_fam["gu1"]] + \
                   [(d_, s_, s_cwd2, B_scr_wd2) for (d_, s_) in cast_fam["wd2"]]

        def bg_cast_step(n=2):
            for _ in range(n):
                if not bg_casts:
                    return
                d_, s_, sem, buf = bg_casts.pop(0)
                P.op("pool", lambda e, d_=d_, s_=s_: e.dma_start(out=d_, in_=s_), dma=sem)
                buf.last_w = (sem, P.dcnt[sem])

        tiles = [[0]] + [[1 + 4 * t + i for i in range(4)] for t in range(8)]
        NT = len(tiles)
        order = [("A", 0)]
        for t in range(NT):
            if t + 1 < NT:
                order.append(("A", t + 1))
            order.append(("L", t))
            if t > 0:
                order.append(("C", t))
        for ph, t in order:
            if ph == "A":
                gu_reqs.extend([(0, j) for j in range(NGRP)])
            elif ph == "C":
                gu_reqs.extend([(1, j) for j in range(NGRP)])
        m1_state = {}
        m1_gens = {}

        def m1_start(t):
            gbs = tiles[t]
            m1_gens[t] = [spawn(m1_tile_gen(len(gbs), gbs[0] * 128))]
            m1_state[t] = "n2"

        def m1_finish(t):
            run_until_done(m1_gens[t])
            transpose_evac(len(tiles[t]), 0, hTa, B_hTa)
            m1_state[t] = "done"

        tails = {"ln": []}

        for pi, (ph, t) in enumerate(order):
            gbs = tiles[t]
            nb = len(gbs)
            par = t % 2
            if ph == "A":
                r0 = gbs[0] * 128
                P.op("pool", lambda e, nb=nb, r0=r0, par=par: e.dma_start(
                    out=xt[par][:, 0:nb, :], in_=x_c[r0:r0 + nb * 128, :].rearrange("(b p) d -> p b d", p=128)),
                    writes=B_x[par][0:nb], dma=s_x[par])
                if m1_state.get(t) is None:
                    run_until_done(tails["ln"])
                    m1_start(t)
                    if t == 0:
                        tick()
                        issue_front_casts()
                if m1_state[t] != "done":
                    m1_finish(t)
                gateup(0, nb, hTa, B_hTa)
                run_until_done(tails["ln"])
                tl = t - 1
                m2 = []
                if tl >= 0:
                    m2 = [spawn(modulate_gen(b, xt[tl % 2][:, b, :], B_x[tl % 2][b])) for b in range(len(tiles[tl]))]
                tails["ln"] = down(nb, 0, False, 0, par)
                if tl >= 0:
                    run_until_done(m2)
                    transpose_evac(len(tiles[tl]), 1, hTa, B_hTa)
                    m1_state[("m2", tl)] = True
            elif ph == "L":
                halo = (t == 0)
                load_mixer_w()
                if not m1_state.get(("m2", t)):
                    m2 = [spawn(modulate_gen(b, xt[par][:, b, :], B_x[par][b])) for b in range(nb)]
                    run_until_done(m2)
                    transpose_evac(nb, 1, hTa, B_hTa)
                w_in_proj(nb, gbs, not halo, hTa, B_hTa)
                if halo:
                    continue
                mix_scores(0, gbs[0])
                mix_scores(1, gbs[1])
                run_until_done(tails["ln"])
                gens = []
                for b in range(nb):
                    mix_pv(b, gbs[b])
                    if b + 2 < nb:
                        mix_scores(b + 2, gbs[b + 2])
                    mix_out(b, gbs[b], par)
                    gens.append(spawn(post_ln_gen(b, 1, False, 0, par, True)))
                    tick()
                    tick()
                run_until_done(gens)
                transpose_evac(nb, 2, hTb, B_hTb)
            else:
                nxt = [tt for (pp, tt) in order[pi + 1:] if pp == "A"]
                hook = None
                if nxt and m1_state.get(nxt[0]) is None:
                    tn = nxt[0]
                    m1_start(tn)

                    def hook(tn=tn):
                        if m1_state[tn] == "n2" and not any(g in live for g in m1_gens[tn]):
                            transpose_evac(len(tiles[tn]), 0, hTa, B_hTa)
                            m1_state[tn] = "done"
                gateup(1, nb, hTb, B_hTb, hook)
                if nxt and m1_state[nxt[0]] != "done":
                    m1_finish(nxt[0])
                tails["ln"] = down(nb, 2, True, (gbs[0] - 1) * 128, par)
        drain()

        P.op("pool", lambda e: e.nop(), waits=out_events)
        with nc.Block() as blk:
            P.flush(blk)
    return nc


_NC_CACHE = {}


def _consts():
    ident = np.eye(128, dtype=np.float32)
    j = np.arange(128)[:, None]
    i = np.arange(128)[None, :]
    m_cur = np.where(j <= i, 0.0, NEG).astype(np.float32)
    m_prev = np.where(j > i, 0.0, NEG).astype(np.float32)
    m_none = np.full((128, 128), NEG, np.float32)

    def tile4(m):
        return m
    masks_gen = np.stack([tile4(m_cur), tile4(m_prev), tile4(m_prev)], axis=1)
    masks_first = np.stack([tile4(m_cur), tile4(m_prev), tile4(m_none)], axis=1)
    a_gen = np.zeros((2, 4, 128, 128), np.float32)
    a_first = np.zeros((2, 4, 128, 128), np.float32)
    for g, w in enumerate(POOL_WINDOWS):
        for t in range(128):
            for tpp in range(t - w + 1, t + 1):
                if tpp >= 0:
                    a_gen[1, g, tpp, t] += 1.0 / w
                else:
                    a_gen[0, g, tpp + 128, t] += 1.0 / w
            a_gen[1, g, t, t] -= 1.0
            cnt = min(t + 1, w)
            for tpp in range(max(t - w + 1, 0), t + 1):
                a_first[1, g, tpp, t] += 1.0 / cnt
            a_first[1, g, t, t] -= 1.0
    return ident, masks_gen, masks_first, a_gen, a_first


def kernel(x, c, w_ada, b_ada, ffn1_w_gate, ffn1_w_up, ffn1_w_down, ln1_g, ln1_b,
           w_in, attn_sinks, pool_w_mix, pool_scale, w_out, ln2_g, ln2_b,
           ffn2_w_gate, ffn2_w_up, ffn2_w_down, ln3_g, ln3_b):
    f32 = lambda a: np.ascontiguousarray(np.asarray(a, dtype=np.float32))
    x = f32(x)
    c = f32(c)
    if "nc" not in _NC_CACHE:
        _NC_CACHE["nc"] = build_program()
    nc = _NC_CACHE["nc"]
    ident, masks_gen, masks_first, a_gen, a_first = _consts()
    bf = ml_dtypes.bfloat16
    shared = {
        "w_ada": f32(w_ada[0]), "b_ada": f32(b_ada[0]).reshape(1, -1),
        "ffn1_w_gate": f32(ffn1_w_gate[0]), "ffn1_w_up": f32(ffn1_w_up[0]), "ffn1_w_down": f32(ffn1_w_down[0]),
        "ffn2_w_gate": f32(ffn2_w_gate[0]), "ffn2_w_up": f32(ffn2_w_up[0]), "ffn2_w_down": f32(ffn2_w_down[0]),
        "ln1_g": f32(ln1_g[0]).reshape(1, -1), "ln1_b": f32(ln1_b[0]).reshape(1, -1),
        "ln2_g": f32(ln2_g[0]).reshape(1, -1), "ln2_b": f32(ln2_b[0]).reshape(1, -1),
        "ln3_g": f32(ln3_g[0]).reshape(1, -1), "ln3_b": f32(ln3_b[0]).reshape(1, -1),
        "w_in": f32(w_in[0]), "w_out": f32(w_out[0]),
        "sinks_b": np.ascontiguousarray(np.broadcast_to(f32(attn_sinks[0]).reshape(1, 8), (128, 8))),
        "pool_w_mix": f32(pool_w_mix[0]),
        "pool_scale_p": np.ascontiguousarray(f32(pool_scale[0]).reshape(4, 128).T),
        "ident_bf": ident.astype(bf), "ident_f": ident,
    }
    in_maps = []
    for core in range(8):
        b, half = core // 2, core % 2
        own = x[b, half * TOK_OWN:(half + 1) * TOK_OWN]
        halo = x[b, TOK_OWN - 128:TOK_OWN] if half == 1 else x[b, 0:128]
        m = dict(shared)
        m["x_c"] = np.ascontiguousarray(np.concatenate([halo, own], axis=0))
        m["c_c"] = np.ascontiguousarray(c[b].reshape(8, 128).T)
        m["masks"] = (masks_first if half == 0 else masks_gen).astype(bf)
        af = a_first if half == 0 else a_gen
        am = np.concatenate([af.reshape(8, 128, 128), a_gen.reshape(8, 128, 128)], axis=0)
        m["amat"] = np.ascontiguousarray(am.transpose(1, 0, 2)).astype(bf)
        in_maps.append(m)
    if _NC_CACHE.get("maps_only"):
        return in_maps
    res = run_bass_kernel_spmd(nc, in_maps, core_ids=list(range(8)))
    out = np.empty((4, SEQ, D), np.float32)
    for core in range(8):
        b, half = core // 2, core % 2
        out[b, half * TOK_OWN:(half + 1) * TOK_OWN] = np.asarray(res.results[core]["out_c"], dtype=np.float32)
    return out
```
